# Optimizing a Trainium2 kernel written in Bass

```python
import math
import jax, jax.numpy as jnp
from jax import lax
import numpy as np

D_MODEL = 1024
BATCH = 16
SEQ = 256
DEPTH = 2
DEC_BATCH = 2
DEC_SEQ = 4096
PAST_LEN = 256

GRID_W = 64
EPS = 1e-6
N_MOD = 6
D_MIX = D_MODEL
A_WIDTH = D_MODEL // 4
B_WIDTH = D_MODEL // 2
C_WIDTH = D_MODEL // 4
SSM_CG = 16
SSM_G = A_WIDTH // SSM_CG
SSM_P = 64
N_DIR = 2
MLA_HEADS = 8
MLA_NOPE = 64
MLA_ROPE = 32
MLA_V = B_WIDTH // MLA_HEADS
MLA_QK = MLA_NOPE + MLA_ROPE
Q_RANK = D_MODEL // 4
KV_RANK = D_MODEL // 8
ROPE_BASE = 10000.0
Q_BLOCK = 128
ATTN_SCALE = 1.0 / math.sqrt(MLA_QK)
GMLP_HEADS = 4
GMLP_CH = C_WIDTH // GMLP_HEADS
CHUNK = 128
OFF_SSM = 0
OFF_Q = OFF_SSM + A_WIDTH
OFF_KV = OFF_Q + Q_RANK
OFF_KR = OFF_KV + KV_RANK
OFF_GM = OFF_KR + MLA_ROPE
IN_COLS = OFF_GM + 2 * C_WIDTH
D_FF = 2816

kernel_name = 'hybrid_s5_mla_gmlp_prefix_dit_step'

F32 = jnp.float32


def _rms(x):
    xf = x.astype(F32)
    return xf * lax.rsqrt(jnp.mean(xf * xf, axis=-1, keepdims=True) + EPS)


def rmsnorm(x, g):
    return (_rms(x) * g.astype(F32)).astype(x.dtype)


def modulate(x, shift, scale):
    return (_rms(x) * (1.0 + scale.astype(F32)) + shift.astype(F32)).astype(x.dtype)


def modulation(cond, w_mod, b_mod):
    m = jax.nn.silu(cond) @ w_mod + b_mod
    m = m.reshape(m.shape[0], 1, N_MOD, D_MODEL)
    return tuple(m[:, :, i] for i in range(N_MOD))


def axial_rope_tables(n_tokens):
    rows = n_tokens // GRID_W
    row = jnp.repeat(jnp.arange(rows, dtype=F32), GRID_W)
    col = jnp.tile(jnp.arange(GRID_W, dtype=F32), rows)
    n_freq = MLA_ROPE // 4
    inv = ROPE_BASE ** (-jnp.arange(n_freq, dtype=F32) / n_freq)
    ang = jnp.concatenate([row[:, None] * inv, col[:, None] * inv], axis=-1)
    return jnp.cos(ang), jnp.sin(ang)


def apply_rope(x, cos, sin):
    x1, x2 = jnp.split(x.astype(F32), 2, axis=-1)
    c = cos[None, :, None, :]
    s = sin[None, :, None, :]
    return jnp.concatenate([x1 * c - x2 * s, x2 * c + x1 * s], axis=-1).astype(x.dtype)


def _scan_combine(e1, e2):
    a1, b1 = e1
    a2, b2 = e2
    return a1 * a2, a2 * b1 + b2


def ssm_direction(u, h0, a_re, a_im, b_re, b_im, c_re, c_im, log_dt, reverse):
    A = lax.complex(a_re.astype(F32), a_im.astype(F32))
    Bm = lax.complex(b_re.astype(F32), b_im.astype(F32))
    Cm = lax.complex(c_re.astype(F32), c_im.astype(F32))
    dt = jnp.exp(log_dt.astype(F32))[:, None]
    a_bar = jnp.exp(A * dt)
    b_bar = ((a_bar - 1.0) / A)[..., None] * Bm
    bu = jnp.einsum('gpc,blgc->blgp', b_bar, u.astype(jnp.complex64))
    edge = -1 if reverse else 0
    bu = bu.at[:, edge].add(a_bar[None] * h0)
    a = jnp.broadcast_to(a_bar, bu.shape)
    _, h = lax.associative_scan(_scan_combine, (a, bu), axis=1, reverse=reverse)
    y = jnp.einsum('gcp,blgp->blgc', Cm, h).real
    h_final = h[:, 0] if reverse else h[:, -1]
    return y, h_final


def mixer_ssm(u, h0_re, h0_im, p):
    bn, L, _ = u.shape
    uf = u.astype(F32).reshape(bn, L, SSM_G, SSM_CG)
    h0 = lax.complex(h0_re.astype(F32), h0_im.astype(F32))
    ys, hs = [], []
    for d in range(N_DIR):
        y_d, h_d = ssm_direction(uf, h0[:, d], p['ssm_a_re'][d], p['ssm_a_im'][d], p['ssm_b_re'][d],
                                 p['ssm_b_im'][d], p['ssm_c_re'][d], p['ssm_c_im'][d], p['ssm_log_dt'][d],
                                 reverse=(d == 1))
        ys.append(y_d)
        hs.append(h_d)
    y = (ys[0] + ys[1]).reshape(bn, L, A_WIDTH) + p['ssm_d'].astype(F32) * u.astype(F32)
    y = jax.nn.gelu(y).astype(u.dtype)
    y = y * jax.nn.sigmoid(y @ p['ssm_w_glu'])
    h_fin = jnp.stack(hs, axis=1)
    return y, h_fin.real, h_fin.imag


def mla_queries(c_q, p, rope):
    bn, L, _ = c_q.shape
    q = (rmsnorm(c_q, p['q_a_norm']) @ p['w_uq']).reshape(bn, L, MLA_HEADS, MLA_QK)
    q = rmsnorm(q, p['q_norm'])
    if rope is not None:
        q = jnp.concatenate([q[..., :MLA_NOPE], apply_rope(q[..., MLA_NOPE:], *rope)], axis=-1)
    return q


def mla_keys_values(ckv, k_rope, p, rope):
    bn, L, _ = ckv.shape
    kv = (ckv @ p['w_ukv']).reshape(bn, L, MLA_HEADS, MLA_NOPE + MLA_V)
    k_nope, v = kv[..., :MLA_NOPE], kv[..., MLA_NOPE:]
    kr = jnp.broadcast_to(k_rope[:, :, None, :], (bn, L, MLA_HEADS, MLA_ROPE)).astype(k_nope.dtype)
    k = rmsnorm(jnp.concatenate([k_nope, kr], axis=-1), p['k_norm'])
    if rope is not None:
        k = jnp.concatenate([k[..., :MLA_NOPE], apply_rope(k[..., MLA_NOPE:], *rope)], axis=-1)
    return k, v


def block_attention(q, k, v):
    bn, L, H, dk = q.shape
    nb = L // Q_BLOCK
    qb = q.reshape(bn, nb, Q_BLOCK, H, dk).transpose(1, 0, 2, 3, 4)

    def one_block(q_blk):
        s = jnp.einsum('bqhd,bkhd->bhqk', q_blk, k).astype(F32) * ATTN_SCALE
        w = jax.nn.softmax(s, axis=-1).astype(v.dtype)
        return jnp.einsum('bhqk,bkhd->bqhd', w, v)

    out = lax.map(one_block, qb)
    return out.transpose(1, 0, 2, 3, 4).reshape(bn, L, H * MLA_V)


def mixer_gmlp(uv, p):
    bn, L, _ = uv.shape
    u, v = uv[..., :C_WIDTH], uv[..., C_WIDTH:]
    v = rmsnorm(v, p['gmlp_v_norm']).reshape(bn, L // CHUNK, CHUNK, GMLP_HEADS, GMLP_CH)
    mixed = jnp.einsum('hqk,bnkhc->bnqhc', p['gmlp_w_s'], v) + p['gmlp_b_s'].T[None, None, :, :, None]
    return u * mixed.reshape(bn, L, C_WIDTH)


def merge_heads(y_a, y_b, y_c, g, w_out):
    y = jnp.concatenate([_rms(y_a), _rms(y_b), _rms(y_c)], axis=-1) * g.astype(F32)
    return y.astype(y_a.dtype) @ w_out


def conv_ffn(h, p):
    up = h @ p['ffn_w_up']
    L = up.shape[1]
    pad = jnp.pad(up, ((0, 0), (1, 1), (0, 0)))
    w = p['ffn_conv_w']
    conv = pad[:, :L] * w[0] + pad[:, 1:L + 1] * w[1] + pad[:, 2:] * w[2] + p['ffn_conv_b']
    gate, val = conv[..., :D_FF], conv[..., D_FF:]
    return (jax.nn.silu(gate) * val) @ p['ffn_w_down']


def trunk_layer(x, mod, p, ctx):
    shift1, scale1, gate1, shift2, scale2, gate2 = mod
    bn, L, _ = x.shape
    h = modulate(x, shift1, scale1)
    z = h @ p['w_in']
    u_ssm = z[..., OFF_SSM:OFF_Q]
    c_q = z[..., OFF_Q:OFF_KV]
    ckv = rmsnorm(z[..., OFF_KV:OFF_KR], p['kv_a_norm'])
    k_rope = z[..., OFF_KR:OFF_GM]
    uv = z[..., OFF_GM:]
    if ctx is None:
        zeros = jnp.zeros((bn, N_DIR, SSM_G, SSM_P), F32)
        y_a, h_re, h_im = mixer_ssm(u_ssm, zeros, zeros, p)
        q = mla_queries(c_q, p, None)
        k, v = mla_keys_values(ckv, k_rope, p, None)
        new_ctx = (ckv, k_rope, h_re, h_im)
    else:
        ckv_c, kr_c, h0_re, h0_im = ctx
        rope = axial_rope_tables(L)
        y_a, _, _ = mixer_ssm(u_ssm, h0_re, h0_im, p)
        q = mla_queries(c_q, p, rope)
        k_l, v_l = mla_keys_values(ckv, k_rope, p, rope)
        k_c, v_c = mla_keys_values(ckv_c.astype(ckv.dtype), kr_c, p, None)
        k = jnp.concatenate([k_c, k_l], axis=1)
        v = jnp.concatenate([v_c.astype(v_l.dtype), v_l], axis=1)
        new_ctx = None
    y_b = block_attention(q, k, v)
    y_c = mixer_gmlp(uv, p)
    x = x + gate1 * merge_heads(y_a, y_b, y_c, p['w_out_norm'], p['w_out'])
    x = x + gate2 * conv_ffn(modulate(x, shift2, scale2), p)
    return x, new_ctx


def setup_inputs(seed: int = 0) -> dict:
    key = jax.random.key(seed)
    ks = iter(jax.random.split(key, 48))

    def nrm(shape, s):
        return jax.random.normal(next(ks), shape, F32) * s

    def gain(shape):
        return 1.0 + nrm(shape, 0.02)

    n_idx = jnp.arange(SSM_P, dtype=F32)
    sh_a = (DEPTH, N_DIR, SSM_G, SSM_P)
    return {
        'x_prompt': nrm((BATCH, SEQ, D_MODEL), 1.0),
        'x_sample': nrm((DEC_BATCH, DEC_SEQ, D_MODEL), 1.0),
        'cache_ckv': nrm((DEC_BATCH, DEPTH, PAST_LEN, KV_RANK), 1.0),
        'cache_krope': nrm((DEC_BATCH, DEPTH, PAST_LEN, MLA_ROPE), 1.0),
        'state_ssm_re': nrm((DEC_BATCH, DEPTH, N_DIR, SSM_G, SSM_P), 0.3),
        'state_ssm_im': nrm((DEC_BATCH, DEPTH, N_DIR, SSM_G, SSM_P), 0.3),
        'c': nrm((DEC_BATCH, D_MODEL), 1.0),
        'c_ctx': nrm((D_MODEL,), 1.0),
        'w_mod': nrm((DEPTH, D_MODEL, N_MOD * D_MODEL), D_MODEL ** -0.5),
        'b_mod': nrm((DEPTH, N_MOD * D_MODEL), 0.02),
        'w_in': nrm((DEPTH, D_MODEL, IN_COLS), D_MODEL ** -0.5),
        'ssm_a_re': -0.5 + nrm(sh_a, 0.01),
        'ssm_a_im': math.pi * n_idx + nrm(sh_a, 0.01),
        'ssm_b_re': nrm((DEPTH, N_DIR, SSM_G, SSM_P, SSM_CG), (2.0 * SSM_CG) ** -0.5),
        'ssm_b_im': nrm((DEPTH, N_DIR, SSM_G, SSM_P, SSM_CG), (2.0 * SSM_CG) ** -0.5),
        'ssm_c_re': nrm((DEPTH, N_DIR, SSM_G, SSM_CG, SSM_P), (2.0 * SSM_P) ** -0.5),
        'ssm_c_im': nrm((DEPTH, N_DIR, SSM_G, SSM_CG, SSM_P), (2.0 * SSM_P) ** -0.5),
        'ssm_log_dt': jax.random.uniform(next(ks), (DEPTH, N_DIR, SSM_G), F32, math.log(1e-3), math.log(1e-1)),
        'ssm_d': nrm((DEPTH, A_WIDTH), 0.5),
        'ssm_w_glu': nrm((DEPTH, A_WIDTH, A_WIDTH), A_WIDTH ** -0.5),
        'q_a_norm': gain((DEPTH, Q_RANK)),
        'kv_a_norm': gain((DEPTH, KV_RANK)),
        'w_uq': nrm((DEPTH, Q_RANK, MLA_HEADS * MLA_QK), Q_RANK ** -0.5),
        'w_ukv': nrm((DEPTH, KV_RANK, MLA_HEADS * (MLA_NOPE + MLA_V)), KV_RANK ** -0.5),
        'q_norm': gain((DEPTH, MLA_QK)),
        'k_norm': gain((DEPTH, MLA_QK)),
        'gmlp_v_norm': gain((DEPTH, C_WIDTH)),
        'gmlp_w_s': nrm((DEPTH, GMLP_HEADS, CHUNK, CHUNK), CHUNK ** -0.5),
        'gmlp_b_s': 1.0 + nrm((DEPTH, GMLP_HEADS, CHUNK), 0.02),
        'w_out_norm': gain((DEPTH, D_MIX)),
        'w_out': nrm((DEPTH, D_MIX, D_MODEL), D_MIX ** -0.5),
        'ffn_w_up': nrm((DEPTH, D_MODEL, 2 * D_FF), D_MODEL ** -0.5),
        'ffn_conv_w': nrm((DEPTH, 3, 2 * D_FF), 3.0 ** -0.5),
        'ffn_conv_b': nrm((DEPTH, 2 * D_FF), 0.02),
        'ffn_w_down': nrm((DEPTH, D_FF, D_MODEL), D_FF ** -0.5),
    }


def reference(x_prompt, x_sample, cache_ckv, cache_krope, state_ssm_re, state_ssm_im, c, c_ctx,
              w_mod, b_mod, w_in, ssm_a_re, ssm_a_im, ssm_b_re, ssm_b_im, ssm_c_re, ssm_c_im,
              ssm_log_dt, ssm_d, ssm_w_glu, q_a_norm, kv_a_norm, w_uq, w_ukv, q_norm, k_norm,
              gmlp_v_norm, gmlp_w_s, gmlp_b_s, w_out_norm, w_out, ffn_w_up, ffn_conv_w, ffn_conv_b,
              ffn_w_down):
    xp = x_prompt
    xs = x_sample
    ckv_list, kr_list, hre_list, him_list = [], [], [], []
    for l in range(DEPTH):
        p = {
            'w_in': w_in[l], 'ssm_a_re': ssm_a_re[l], 'ssm_a_im': ssm_a_im[l], 'ssm_b_re': ssm_b_re[l],
            'ssm_b_im': ssm_b_im[l], 'ssm_c_re': ssm_c_re[l], 'ssm_c_im': ssm_c_im[l],
            'ssm_log_dt': ssm_log_dt[l], 'ssm_d': ssm_d[l], 'ssm_w_glu': ssm_w_glu[l],
            'q_a_norm': q_a_norm[l], 'kv_a_norm': kv_a_norm[l], 'w_uq': w_uq[l], 'w_ukv': w_ukv[l],
            'q_norm': q_norm[l], 'k_norm': k_norm[l], 'gmlp_v_norm': gmlp_v_norm[l],
            'gmlp_w_s': gmlp_w_s[l], 'gmlp_b_s': gmlp_b_s[l], 'w_out_norm': w_out_norm[l],
            'w_out': w_out[l], 'ffn_w_up': ffn_w_up[l], 'ffn_conv_w': ffn_conv_w[l],
            'ffn_conv_b': ffn_conv_b[l], 'ffn_w_down': ffn_w_down[l],
        }
        mod_ctx = modulation(c_ctx[None, :], w_mod[l], b_mod[l])
        xp, (ckv_l, kr_l, hre_l, him_l) = trunk_layer(xp, mod_ctx, p, None)
        ckv_list.append(ckv_l)
        kr_list.append(kr_l)
        hre_list.append(hre_l)
        him_list.append(him_l)
        mod_lat = modulation(c, w_mod[l], b_mod[l])
        xs, _ = trunk_layer(xs, mod_lat, p,
                            (cache_ckv[:, l], cache_krope[:, l], state_ssm_re[:, l], state_ssm_im[:, l]))
    new_ckv = jnp.stack(ckv_list, axis=1)
    new_krope = jnp.stack(kr_list, axis=1)
    new_ssm_re = jnp.stack(hre_list, axis=1)
    new_ssm_im = jnp.stack(him_list, axis=1)
    return (xp, xs, new_ckv, new_krope, new_ssm_re, new_ssm_im)
```

```python
import math
from contextlib import ExitStack
import numpy as np
import concourse.bass as bass
import concourse.mybir as mybir
from concourse.bass_utils import run_bass_kernel_spmd

F32 = mybir.dt.float32
BF16 = mybir.dt.bfloat16
AF = mybir.ActivationFunctionType
ALU = mybir.AluOpType
AX = mybir.AxisListType

D = 1024
DEPTH = 2
SEQ = 256
NSEQ_P = 2
DEC_SEQ = 4096
PAST = 256
GRID_W = 64
EPS = 1e-6
DFF = 2816
NF = DFF // 128
ATTN_SCALE = 1.0 / math.sqrt(96.0)
N_CORES = 8

SAME_ENGINE_SYNC = True
N_DMA_SEMS = {"sp": 56, "pool": 24}


class Res:
    __slots__ = ("name", "last_w", "readers")

    def __init__(self, name=""):
        self.name = name
        self.last_w = None
        self.readers = []


class Op:
    __slots__ = ("eng", "fn", "deps", "dma", "signal", "ticket", "idx", "prewait")

    def __init__(self, eng, fn, dma):
        self.eng = eng
        self.fn = fn
        self.dma = dma
        self.deps = set()
        self.signal = False
        self.ticket = None
        self.prewait = None


class Prog:
    ENGS = ("pe", "act", "dve", "pool", "sp")

    def __init__(self, nc):
        self.nc = nc
        self.ops = []

    def op(self, eng, fn, reads=(), writes=(), dma=False):
        o = Op(eng, fn, dma)
        o.idx = len(self.ops)
        deps = set()
        for r in reads:
            if r.last_w is not None:
                deps.add(r.last_w)
        for w in writes:
            if w.last_w is not None:
                deps.add(w.last_w)
            deps.update(w.readers)
        for d in deps:
            od = self.ops[d]
            if od.eng == eng and not od.dma:
                if eng == "pe" or not SAME_ENGINE_SYNC:
                    continue
            o.deps.add(d)
            od.signal = True
        for r in reads:
            r.readers.append(o.idx)
        for w in writes:
            w.last_w = o.idx
            w.readers = []
        self.ops.append(o)
        return o

    def emit(self, stack):
        nc = self.nc
        csem = {}
        for e in ("pe", "act", "dve", "pool"):
            csem[e] = stack.enter_context(nc.semaphore("c_" + e))
        dsems = {}
        for e in ("sp", "pool"):
            dsems[e] = [stack.enter_context(nc.semaphore("d_%s_%d" % (e, i))) for i in range(N_DMA_SEMS[e])]
        ccount = {e: 0 for e in csem}
        dcount = {e: 0 for e in dsems}
        duse = {e: [0] * N_DMA_SEMS[e] for e in dsems}
        for o in self.ops:
            if o.dma:
                k = dcount[o.eng] % N_DMA_SEMS[o.eng]
                dcount[o.eng] += 1
                s = dsems[o.eng][k]
                prev = duse[o.eng][k]
                if prev > 0:
                    o.prewait = (s, 16 * prev)
                duse[o.eng][k] = prev + 1
                o.ticket = (s, 16 * (prev + 1))
            elif o.signal:
                ccount[o.eng] += 1
                o.ticket = (csem[o.eng], ccount[o.eng])
        self.stats = dict(n_ops=len(self.ops), signals=dict(ccount), dmas=dict(dcount))
        block = stack.enter_context(nc.Block())
        per_eng = {e: [o for o in self.ops if o.eng == e] for e in self.ENGS}
        ops = self.ops

        def run(engobj, ename):
            seen = {}
            for o in per_eng[ename]:
                waits = []
                if o.prewait is not None:
                    waits.append(o.prewait)
                for d in sorted(o.deps):
                    waits.append(ops[d].ticket)
                for (s, v) in waits:
                    key = id(s)
                    if seen.get(key, 0) >= v:
                        continue
                    seen[key] = v
                    engobj.wait_ge(s, v)
                ins = o.fn(engobj)
                if o.ticket is not None:
                    ins.then_inc(o.ticket[0], 16 if o.dma else 1)

        @block.sync
        def _(e):
            run(e, "sp")

        @block.scalar
        def _(e):
            run(e, "act")

        @block.vector
        def _(e):
            run(e, "dve")

        @block.gpsimd
        def _(e):
            run(e, "pool")

        @block.tensor
        def _(e):
            run(e, "pe")


VOFF = {}
_o = 0
for _n, _w in [("bmod", 48), ("ssmd", 2), ("qan", 2), ("kvan", 1), ("qn", 1), ("kn", 1), ("won", 8),
               ("convw", 132), ("convb", 44), ("gbias", 256), ("gvn", 256), ("are", 16), ("aim", 16), ("ldt", 16)]:
    VOFF[_n] = _o
    _o += _w
NV = _o


class Part:
    def __init__(self, name, nseq, L, ctx):
        self.name = name
        self.nseq = nseq
        self.L = L
        self.ctx = ctx
        self.N = nseq * L
        self.kvoff = PAST if ctx else 0
        self.NKV = self.N + self.kvoff


def build_program(dec_seq=DEC_SEQ):
    nc = bass.Bass("TRN2", target_bir_lowering=False)
    parts = [Part("p", NSEQ_P, SEQ, False), Part("s", 1, dec_seq, True)]
    NMAX = max(p.N for p in parts)
    NKVMAX = max(p.NKV for p in parts)
    st = ExitStack()
    with st:
        P = Prog(nc)
        st.enter_context(nc.allow_non_contiguous_dma(reason="tiny halo / state / per-feature vector transfers"))

        def din(name, shape, dt=F32):
            return nc.dram_tensor(name, list(shape), dt, kind="ExternalInput").ap()

        def dout(name, shape, dt=F32):
            return nc.dram_tensor(name, list(shape), dt, kind="ExternalOutput").ap()

        def dscr(name, shape, dt):
            return nc.dram_tensor(name, list(shape), dt, kind="Internal").ap()

        def sb(name, shape, dt):
            return st.enter_context(nc.sbuf_tensor("sb_" + name, list(shape), dt))

        xin = {p.name: din("xT_" + p.name, [D, p.N]) for p in parts}
        yout = {p.name: dout("yT_" + p.name, [D, p.N]) for p in parts}
        condT = din("condT", [128, 8, 2])
        cache_ckvT = din("cache_ckvT", [DEPTH, 128, PAST])
        cache_krT = din("cache_krT", [DEPTH, 32, PAST])
        h0in = din("h0", [DEPTH, 128, 16, 2])
        ropeC = din("ropeC", [32, dec_seq])
        ropeS = din("ropeS", [32, dec_seq])
        consts_in = din("consts", [128, 256])
        vecs_in = din("vecs", [DEPTH, 128, NV])
        w_mod_in = din("w_mod_t", [DEPTH, 24, 128, 8 * 256])
        w_in_in = din("w_in_t", [DEPTH, 128, 8 * 1184])
        w_out_in = din("w_out_t", [DEPTH, 128, 8 * 1024])
        w_uq_in = din("w_uq_t", [DEPTH, 128, 2 * 768])
        w_ukvk_in = din("w_ukvk_t", [DEPTH, 128, 8 * 96])
        w_ukvv_in = din("w_ukvv_t", [DEPTH, 128, 512])
        w_glu_in = din("w_glu_t", [DEPTH, 128, 2 * 256])
        w_up_in = din("w_up_t", [DEPTH, NF, 128, 8 * 256])
        w_down_in = din("w_down_t", [DEPTH, 8, 128, NF * 128])
        ssmB_in = din("ssmB_t", [DEPTH, 128, 16 * 2 * 128])
        ssmC_in = din("ssmC_t", [DEPTH, 128, 16 * 2 * 128])
        gws_in = din("gmlp_wsT", [DEPTH, 128, 4 * 128])
        new_ckvT = dout("new_ckvT", [DEPTH, 128, parts[0].N])
        new_krT = dout("new_krT", [DEPTH, 32, parts[0].N])
        new_h = dout("new_h", [DEPTH, NSEQ_P, 128, 16, 2])

        w_mod_b = dscr("w_mod_b", [DEPTH, 24, 128, 8 * 256], BF16)
        w_in_b = dscr("w_in_b", [DEPTH, 128, 8 * 1184], BF16)
        w_out_b = dscr("w_out_b", [DEPTH, 128, 8 * 1024], BF16)
        w_uq_b = dscr("w_uq_b", [DEPTH, 128, 2 * 768], BF16)
        w_up_b = dscr("w_up_b", [DEPTH, NF, 128, 8 * 256], BF16)
        w_down_b = dscr("w_down_b", [DEPTH, 8, 128, NF * 128], BF16)
        ssmB_b = dscr("ssmB_b", [DEPTH, 128, 16 * 2 * 128], BF16)
        ssmC_b = dscr("ssmC_b", [DEPTH, 128, 3 * 16 * 128], BF16)
        scr = {}
        for p in parts:
            scr[p.name] = dict(
                u=dscr("s_u_" + p.name, [256, p.N], BF16),
                q=dscr("s_q_" + p.name, [8, 96, p.N], BF16),
                yc=dscr("s_yc_" + p.name, [256, p.N], BF16),
                ya=dscr("s_ya_" + p.name, [256, p.N], BF16),
                yb=dscr("s_yb_" + p.name, [512, p.N], BF16),
                yf=dscr("s_yf_" + p.name, [256, p.N], F32),
                h2=dscr("s_h2_" + p.name, [D, p.nseq, p.L + 2], BF16),
                xa=dscr("s_xa_" + p.name, [D, p.N], F32),
                xb=dscr("s_xb_" + p.name, [D, p.N], F32),
            )

        vecs = sb("vecs", [128, DEPTH, NV], F32)
        consts = sb("consts", [128, 256], F32)
        cbf = sb("cbf", [128, 256], BF16)
        ones_b = sb("ones_b", [128, 128], BF16)
        ones_f = sb("ones_f", [128, 128], F32)
        epsc = sb("epsc", [128, 1], F32)
        zeros_b = sb("zeros_b", [128, 16], BF16)
        condsb = sb("condsb", [128, 8, 2], F32)
        condb = sb("condb", [128, 8, 2], BF16)
        modT = sb("modT", [128, 48, 2], F32)
        WB = sb("WB", [128, 11264], BF16)
        wsm = sb("wsm", [128, 8 * 96 + 512 + 512 + 512], BF16)
        U1 = sb("U1", [128, 11264], BF16)
        ring_up = [sb("rup%d" % i, [128, 8 * 256], BF16) for i in range(3)]
        xt = [sb("xt%d" % i, [128, 8, 512], F32) for i in range(2)]
        hb = [sb("hb%d" % i, [128, 8, 512], BF16) for i in range(2)]
        rstd = sb("rstd", [128, 512], F32)
        tmpf = [sb("tmpf%d" % i, [128, 512], F32) for i in range(4)]
        tmpb = [sb("tmpb%d" % i, [128, 512], BF16) for i in range(4)]
        f2 = [sb("f2_%d" % i, [128, 2, 512], F32) for i in range(3)]
        b2 = [sb("b2_%d" % i, [128, 2, 512], BF16) for i in range(3)]
        ckvn_res = sb("ckvn_res", [128, NKVMAX], BF16)
        kr_res = sb("kr_res", [32, NKVMAX], BF16)
        ropec_t = sb("ropec_t", [32, 512], F32)
        ropes_t = sb("ropes_t", [32, 512], F32)
        vpad = sb("vpad", [128, 4, 128], BF16)
        smallc = sb("smallc", [128, 8], F32)
        Tcs = sb("Tcs", [128, 16, 2, 128], F32)
        Tc = Tcs[:, :, 0, :]
        Ts = Tcs[:, :, 1, :]
        sc = {n: sb("ssm_" + n, [128, 16], F32) for n in
              ["r", "cT", "sT", "nsT", "fr", "fi", "for", "foi", "fir", "fii", "t0", "t1", "t2", "t3", "t4", "t5",
               "c", "s", "h0r", "h0i"]}
        rr1 = sb("rr1", [128, 16, 2], F32)
        rr2 = sb("rr2", [128, 16, 2], F32)
        carry = sb("carry", [128, 16, 2], F32)
        hfin = sb("hfin", [128, 16, 2], F32)
        sw = [sb("sw%d" % i, [128, 8, 128], F32) for i in range(2)]
        sq4 = [sb("sq4_%d" % i, [128, 4, 128], BF16) for i in range(3)]
        qtile = [sb("qtile%d" % i, [96, 512], BF16) for i in range(2)]
        ptile = [sb("ptile%d" % i, [128, 512], BF16) for i in range(3)]
        osb = sb("osb", [65, 512], F32)
        pbs = [st.enter_context(nc.psum_tensor("pb%d" % i, [128, 512], F32)) for i in range(8)]

        R = {}

        def rs(name):
            if name not in R:
                R[name] = Res(name)
            return R[name]

        pbR = [rs("pb%d" % i) for i in range(8)]

        def DMA(out, in_, rd, wr, eng="sp"):
            return P.op(eng, lambda e: e.dma_start(out=out, in_=in_), reads=rd, writes=wr, dma=True)

        def MM(out, lhsT, rhs, start, stop, rd, wr):
            return P.op("pe", lambda e: e.matmul(out, lhsT=lhsT, rhs=rhs, start=start, stop=stop), reads=rd, writes=wr)

        def ACT(out, in_, func, rd, wr, scale=None, bias=None):
            kw = {}
            if scale is not None:
                kw["scale"] = scale
            if bias is not None:
                kw["bias"] = bias
            return P.op("act", lambda e: e.activation(out=out, in_=in_, func=func, **kw), reads=rd, writes=wr)

        def TT(eng, out, in0, in1, op, rd, wr):
            return P.op(eng, lambda e: e.tensor_tensor(out=out, in0=in0, in1=in1, op=op), reads=rd, writes=wr)

        def STT(out, in0, scalar, in1, op0, op1, rd, wr):
            return P.op("dve", lambda e: e.scalar_tensor_tensor(out=out, in0=in0, scalar=scalar, in1=in1, op0=op0, op1=op1),
                        reads=rd, writes=wr)

        def TS(eng, out, in0, s1, op0, rd, wr, s2=None, op1=None):
            if op1 is None:
                return P.op(eng, lambda e: e.tensor_scalar(out=out, in0=in0, scalar1=s1, scalar2=None, op0=op0), reads=rd, writes=wr)
            return P.op(eng, lambda e: e.tensor_scalar(out=out, in0=in0, scalar1=s1, scalar2=s2, op0=op0, op1=op1), reads=rd, writes=wr)

        def CP(eng, out, in_, rd, wr):
            return P.op(eng, lambda e: e.tensor_copy(out=out, in_=in_), reads=rd, writes=wr)

        def MS(eng, ap, val, wr):
            return P.op(eng, lambda e: e.memset(ap, val), writes=wr)

        def RECIP(out, in_, rd, wr):
            return P.op("dve", lambda e: e.reciprocal(out=out, in_=in_), reads=rd, writes=wr)

        def rstd_from_ss(ps_ap, out_ap, scale, rd, wr):
            np_ = out_ap.shape[0]
            ACT(out_ap, ps_ap, AF.Ln, rd + [rs("epsc")], wr, scale=scale, bias=epsc[0:np_, 0:1])
            ACT(out_ap, out_ap, AF.Exp, wr, wr, scale=-0.5)

        DMA(consts[:], consts_in, [], [rs("consts")])
        CP("dve", cbf[:], consts[:], [rs("consts")], [rs("cbf")])
        pmat = cbf[0:32, 0:32]
        id3296 = cbf[0:32, 32:128]
        MS("dve", ones_b[:], 1.0, [rs("ones_b")])
        MS("dve", ones_f[:], 1.0, [rs("ones_f")])
        MS("dve", epsc[:], EPS, [rs("epsc")])
        MS("dve", zeros_b[:], 0.0, [rs("zeros_b")])
        MS("pool", vpad[:], 0.0, [rs("vpad")])
        DMA(vecs[:], vecs_in.rearrange("l p v -> p l v"), [], [rs("vecs")])
        DMA(condsb[:], condT, [], [rs("condsb")])
        ACT(condb[:], condsb[:], AF.Silu, [rs("condsb")], [rs("condb")])
        cast_ctr = [0]

        def cast(dst, src, name):
            cast_ctr[0] += 1
            DMA(dst, src, [], [rs(name), rs("castslot%d" % (cast_ctr[0] % 4))], eng="pool")

        def cast_layer(l):
            for j in range(0, 24, 4):
                cast(w_mod_b[l, j:j + 4].rearrange("j p (a m) -> j p a m", a=2),
                     w_mod_in[l, j:j + 4].rearrange("j p (a m) -> j p a m", a=2), "w_mod_b%d_%d" % (l, j // 4))
            cast(w_in_b[l].rearrange("p (k m) -> p k m", k=8), w_in_in[l].rearrange("p (k m) -> p k m", k=8), "w_in_b%d" % l)
            cast(w_uq_b[l].rearrange("p (k m) -> p k m", k=2), w_uq_in[l].rearrange("p (k m) -> p k m", k=2), "w_uq_b%d" % l)
            cast(ssmB_b[l].rearrange("p (a m) -> p a m", a=4), ssmB_in[l].rearrange("p (a m) -> p a m", a=4), "ssmB_b%d" % l)
            cast(w_out_b[l].rearrange("p (k m) -> p k m", k=8), w_out_in[l].rearrange("p (k m) -> p k m", k=8), "w_out_b%d" % l)
            for f in range(0, NF, 2):
                cast(w_up_b[l, f:f + 2].rearrange("j p (a m) -> j p a m", a=2),
                     w_up_in[l, f:f + 2].rearrange("j p (a m) -> j p a m", a=2), "w_up_b%d" % l)
            for o in range(0, 8, 2):
                cast(w_down_b[l, o:o + 2].rearrange("j p (a m) -> j p a m", a=4),
                     w_down_in[l, o:o + 2].rearrange("j p (a m) -> j p a m", a=4), "w_down_b%d" % l)

        cast_layer(0)
        cast_layer(1)

        bank_ctr = [0]
        rope_loaded = [None]
        qctr = [0]
        actr = [0]

        def gbank():
            bank_ctr[0] ^= 1
            return 1 + bank_ctr[0]

        for l in range(DEPTH):
            V = lambda n, i=0, w=1, rows=128, l=l: vecs[0:rows, l, VOFF[n] + i:VOFF[n] + i + w]
            Rv = rs("vecs")
            DMA(wsm[:, 0:768], w_ukvk_in[l], [], [rs("wsm")], eng="pool")
            DMA(wsm[:, 768:1280], w_ukvv_in[l], [], [rs("wsm")], eng="pool")
            DMA(wsm[:, 1280:1792], w_glu_in[l], [], [rs("wsm")], eng="pool")
            DMA(wsm[:, 1792:2304], gws_in[l], [], [rs("wsm")], eng="pool")
            wukvk = wsm[:, 0:768].rearrange("p (h m) -> p h m", h=8)
            wukvv = wsm[:, 768:1280]
            wglu = wsm[:, 1280:1792].rearrange("p (k m) -> p k m", k=2)
            gws = wsm[:, 1792:2304].rearrange("p (h q) -> p h q", h=4)

            pm = pbs[0]
            for jb in range(24):
                rb = ring_up[jb % 3]
                rR = rs("rup%d" % (jb % 3))
                DMA(rb[:], w_mod_b[l, jb], [rs("w_mod_b%d_%d" % (l, jb // 4))], [rR])
                rbv = rb[:].rearrange("p (k m) -> p k m", k=8)
                for hf in range(2):
                    j = jb * 2 + hf
                    for k in range(8):
                        MM(pm[:, 2 * j:2 * j + 2], rbv[:, k, hf * 128:(hf + 1) * 128], condb[:, k, :], k == 0, k == 7,
                           [rR, rs("condb")], [pbR[0]])
            for cnd in range(2):
                TT("dve", modT[:, :, cnd], pm[:, 0:96].rearrange("p (j c) -> p j c", c=2)[:, :, cnd], V("bmod", 0, 48),
                   ALU.add, [pbR[0], Rv], [rs("modT")])
            for i in (1, 4):
                TS("dve", modT[:, i * 8:(i + 1) * 8, :], modT[:, i * 8:(i + 1) * 8, :], 1.0, ALU.add, [rs("modT")], [rs("modT")])

            Rs = rs("ssmc")
            are, aim, ldt = V("are", 0, 16), V("aim", 0, 16), V("ldt", 0, 16)
            S = lambda n: sc[n][:]
            ACT(S("t0"), ldt, AF.Exp, [Rv], [Rs])
            TT("dve", S("t1"), are, S("t0"), ALU.mult, [Rv, Rs], [Rs])
            TT("dve", S("t2"), aim, S("t0"), ALU.mult, [Rv, Rs], [Rs])
            ACT(S("r"), S("t1"), AF.Exp, [Rs], [Rs])
            ACT(S("s"), S("t2"), AF.Sin, [Rs], [Rs], scale=1.0 / 8)
            ACT(S("t3"), S("t2"), AF.Sin, [Rs], [Rs], scale=1.0 / 16)
            TT("dve", S("t3"), S("t3"), S("t3"), ALU.mult, [Rs], [Rs])
            TS("dve", S("c"), S("t3"), -2.0, ALU.mult, [Rs], [Rs], 1.0, ALU.add)

            def csq(cn, sn):
                TT("dve", S("t3"), S(cn), S(sn), ALU.mult, [Rs], [Rs])
                TT("dve", S("t4"), S(cn), S(cn), ALU.mult, [Rs], [Rs])
                TT("dve", S("t5"), S(sn), S(sn), ALU.mult, [Rs], [Rs])
                TT("dve", S(cn), S("t4"), S("t5"), ALU.subtract, [Rs], [Rs])
                TS("dve", S(sn), S("t3"), 2.0, ALU.mult, [Rs], [Rs])

            for _ in range(3):
                csq("c", "s")
            RT = rs("Ttab")
            MS("dve", Tc[:, :, 0:1], 1.0, [RT])
            MS("dve", Ts[:, :, 0:1], 0.0, [RT])
            CP("dve", sc["cT"][:], S("c"), [Rs], [Rs])
            CP("dve", sc["sT"][:], S("s"), [Rs], [Rs])
            n = 1
            while n < 128:
                cb = sc["cT"][:, :, None].to_broadcast([128, 16, n])
                sbb = sc["sT"][:, :, None].to_broadcast([128, 16, n])
                flat = xt[1][:].rearrange("p a b -> p (a b)")
                w0 = flat[:, 0:16 * n].rearrange("p (g t) -> p g t", g=16)
                w1 = flat[:, 2048:2048 + 16 * n].rearrange("p (g t) -> p g t", g=16)
                Rw = rs("xt1")
                TT("dve", w0, Tc[:, :, 0:n], cb, ALU.mult, [RT, Rs], [Rw])
                TT("dve", w1, Ts[:, :, 0:n], sbb, ALU.mult, [RT, Rs], [Rw])
                TT("dve", Tc[:, :, n:2 * n], w0, w1, ALU.subtract, [Rw], [RT])
                TT("dve", w0, Tc[:, :, 0:n], sbb, ALU.mult, [RT, Rs], [Rw])
                TT("dve", w1, Ts[:, :, 0:n], cb, ALU.mult, [RT, Rs], [Rw])
                TT("dve", Ts[:, :, n:2 * n], w0, w1, ALU.add, [Rw], [RT])
                csq("cT", "sT")
                n *= 2
            TS("dve", S("nsT"), S("sT"), -1.0, ALU.mult, [Rs], [Rs])
            CP("dve", rr1[:, :, 0], S("r"), [Rs], [Rs])
            MS("dve", rr1[:, :, 1], 1.0, [Rs])
            TS("dve", rr2[:, :, 0], S("r"), -1.0, ALU.mult, [Rs], [Rs])
            MS("dve", rr2[:, :, 1], -1.0, [Rs])
            TT("dve", S("t0"), S("r"), S("c"), ALU.mult, [Rs], [Rs])
            TS("dve", S("t0"), S("t0"), -1.0, ALU.add, [Rs], [Rs])
            TT("dve", S("t1"), S("r"), S("s"), ALU.mult, [Rs], [Rs])
            TT("dve", S("t2"), are, are, ALU.mult, [Rv], [Rs])
            TT("dve", S("t3"), aim, aim, ALU.mult, [Rv], [Rs])
            TT("dve", S("t2"), S("t2"), S("t3"), ALU.add, [Rs], [Rs])
            RECIP(S("t2"), S("t2"), [Rs], [Rs])
            TT("dve", S("t3"), S("t0"), are, ALU.mult, [Rs, Rv], [Rs])
            TT("dve", S("t4"), S("t1"), aim, ALU.mult, [Rs, Rv], [Rs])
            TT("dve", S("t3"), S("t3"), S("t4"), ALU.add, [Rs], [Rs])
            TT("dve", S("fr"), S("t3"), S("t2"), ALU.mult, [Rs], [Rs])
            TT("dve", S("t3"), S("t1"), are, ALU.mult, [Rs, Rv], [Rs])
            TT("dve", S("t4"), S("t0"), aim, ALU.mult, [Rs, Rv], [Rs])
            TT("dve", S("t3"), S("t3"), S("t4"), ALU.subtract, [Rs], [Rs])
            TT("dve", S("fi"), S("t3"), S("t2"), ALU.mult, [Rs], [Rs])
            TT("dve", S("t3"), S("fr"), S("c"), ALU.mult, [Rs], [Rs])
            TT("dve", S("t4"), S("fi"), S("s"), ALU.mult, [Rs], [Rs])
            TT("dve", S("for"), S("t3"), S("t4"), ALU.add, [Rs], [Rs])
            TT("dve", S("t3"), S("fi"), S("c"), ALU.mult, [Rs], [Rs])
            TT("dve", S("t4"), S("fr"), S("s"), ALU.mult, [Rs], [Rs])
            TT("dve", S("foi"), S("t3"), S("t4"), ALU.subtract, [Rs], [Rs])
            TT("dve", S("t0"), S("fr"), S("fr"), ALU.mult, [Rs], [Rs])
            TT("dve", S("t1"), S("fi"), S("fi"), ALU.mult, [Rs], [Rs])
            TT("dve", S("t0"), S("t0"), S("t1"), ALU.add, [Rs], [Rs])
            RECIP(S("t0"), S("t0"), [Rs], [Rs])
            TT("dve", S("t3"), S("c"), S("fr"), ALU.mult, [Rs], [Rs])
            TT("dve", S("t4"), S("s"), S("fi"), ALU.mult, [Rs], [Rs])
            TT("dve", S("t3"), S("t3"), S("t4"), ALU.add, [Rs], [Rs])
            TT("dve", S("fir"), S("t3"), S("t0"), ALU.mult, [Rs], [Rs])
            TT("dve", S("t3"), S("s"), S("fr"), ALU.mult, [Rs], [Rs])
            TT("dve", S("t4"), S("c"), S("fi"), ALU.mult, [Rs], [Rs])
            TT("dve", S("t3"), S("t3"), S("t4"), ALU.subtract, [Rs], [Rs])
            TT("dve", S("fii"), S("t3"), S("t0"), ALU.mult, [Rs], [Rs])
            cf = xt[0][:].rearrange("p a b -> p (a b)")
            Rx0 = rs("xt0")
            DMA(cf, ssmC_in[l], [], [Rx0])
            cfv = cf.rearrange("p (g r m) -> p g r m", g=16, r=2)
            cov = U1[:, 0:6144].rearrange("p (t g m) -> p t g m", t=3, g=16)
            RU = rs("U1")
            for gd in range(16):
                frc, fic = sc["fr"][:, gd:gd + 1], sc["fi"][:, gd:gd + 1]
                t_a, t_b = tmpf[0][:, 0:128], tmpf[1][:, 0:128]
                Rt = [rs("tmpf0"), rs("tmpf1")]
                TS("dve", t_a, cfv[:, gd, 1, :], fic, ALU.mult, [Rx0, Rs], [Rt[0]])
                STT(t_b, cfv[:, gd, 0, :], frc, t_a, ALU.mult, ALU.subtract, [Rx0, Rs, Rt[0]], [Rt[1]])
                CP("dve", cov[:, 0, gd, :], t_b, [Rt[1]], [RU])
                TS("dve", cov[:, 1, gd, :], t_b, -1.0, ALU.mult, [Rt[1]], [RU])
                TS("dve", t_a, cfv[:, gd, 1, :], frc, ALU.mult, [Rx0, Rs], [Rt[0]])
                STT(t_b, cfv[:, gd, 0, :], fic, t_a, ALU.mult, ALU.add, [Rx0, Rs, Rt[0]], [Rt[1]])
                TS("dve", cov[:, 2, gd, :], t_b, -1.0, ALU.mult, [Rt[1]], [RU])
            DMA(ssmC_b[l], U1[:, 0:6144], [RU], [rs("ssmC_b%d" % l)])

            for p in parts:
                pn = p.name
                N, L = p.N, p.L
                S_ = scr[pn]
                ntile = (N + 511) // 512
                x_src = xin[pn] if l == 0 else S_["xb"]
                x_dst = yout[pn] if l == DEPTH - 1 else S_["xb"]
                md = 0 if not p.ctx else 1
                MOD = lambda i, c, md=md: modT[:, i * 8 + c, md:md + 1]
                Rm = rs("modT")
                tR = lambda nm, t: rs("%s_%s_%d" % (nm, pn, t))

                if p.ctx:
                    DMA(ckvn_res[:, 0:PAST], cache_ckvT[l], [], [rs("ckvres_c")], eng="pool")
                    DMA(kr_res[:, 0:PAST], cache_krT[l], [], [rs("krres_c")], eng="pool")

                RW = rs("WB")
                DMA(WB[:, 0:8 * 1184], w_in_b[l], [rs("w_in_b%d" % l)], [RW])
                DMA(WB[:, 8 * 1184:8 * 1184 + 1536], w_uq_b[l], [rs("w_uq_b%d" % l)], [RW])
                winT = WB[:, 0:8 * 1184].rearrange("p (k m) -> p k m", k=8)
                wuq = WB[:, 8 * 1184:8 * 1184 + 1536].rearrange("p (k m) -> p k m", k=2)

                def qk_gen(produce_raw, gaincol, out_ap, outR, n, rope_pos, par, finish, extraW=(), banks=((3, 4), (6, 7))):
                    pq_b, aux_b = banks[par]
                    psrc, psR = pbs[pq_b][0:96, 0:n], pbR[pq_b]
                    produce_raw(psrc, psR)
                    yield
                    sqk = tmpb[par][0:96, 0:n]
                    Rsqk = rs("tmpb%d" % par)
                    ACT(sqk, psrc, AF.Square, [psR], [Rsqk])
                    yield
                    MM(pbs[aux_b][0:96, 0:n], ones_b[0:96, 0:96], sqk, True, True, [Rsqk, rs("ones_b")], [pbR[aux_b]])
                    yield
                    ri_ = 2 if par == 0 else 0
                    rsq = tmpf[ri_][0:96, 0:n]
                    Rrsq = rs("tmpf%d" % ri_)
                    rstd_from_ss(pbs[aux_b][0:96, 0:n], rsq, 1.0 / 96, [pbR[aux_b]], [Rrsq])
                    yield
                    STT(out_ap, psrc, gaincol, rsq, ALU.mult, ALU.mult, [psR, Rv, Rrsq], [outR] + list(extraW))
                    yield
                    if rope_pos is not None:
                        if rope_loaded[0] != (rope_pos, n):
                            rope_loaded[0] = (rope_pos, n)
                            DMA(ropec_t[:, 0:n], ropeC[:, rope_pos:rope_pos + n], [], [rs("ropec")])
                            DMA(ropes_t[:, 0:n], ropeS[:, rope_pos:rope_pos + n], [], [rs("ropes")])
                        MM(pbs[aux_b][0:32, 0:n], pmat, out_ap[0:32, :], True, True, [outR, rs("cbf")], [pbR[aux_b]])
                        yield
                        ti_ = 3 if par == 0 else 1
                        t1 = tmpf[ti_][0:32, 0:n]
                        t2 = tmpf[ri_][0:32, 0:n]
                        TT("dve", t1, out_ap[0:32, :], ropec_t[:, 0:n], ALU.mult, [outR, rs("ropec")], [rs("tmpf%d" % ti_)])
                        TT("dve", t2, pbs[aux_b][0:32, 0:n], ropes_t[:, 0:n], ALU.mult, [pbR[aux_b], rs("ropes")], [Rrsq])
                        TT("dve", out_ap[0:32, :], t1, t2, ALU.add, [rs("tmpf%d" % ti_), Rrsq], [outR])
                        yield
                    finish()

                def run_rr_gen(gen_makers):
                    free = [0, 1]
                    active = []
                    pendm = list(gen_makers)
                    while pendm or active:
                        while pendm and free:
                            par = free.pop(0)
                            active.append((par, pendm.pop(0)(par)))
                        for (par, g) in list(active):
                            try:
                                next(g)
                            except StopIteration:
                                active.remove((par, g))
                                free.append(par)
                        yield

                def drive(gA, gB, nB):
                    doneA = doneB = False
                    while not (doneA and doneB):
                        if not doneA:
                            try:
                                next(gA)
                            except StopIteration:
                                doneA = True
                        for _ in range(nB):
                            if doneB:
                                break
                            try:
                                next(gB)
                            except StopIteration:
                                doneB = True

                def run_rr(gen_makers):
                    free = [0, 1]
                    active = []
                    pendm = list(gen_makers)
                    while pendm or active:
                        while pendm and free:
                            par = free.pop(0)
                            active.append((par, pendm.pop(0)(par)))
                        for (par, g) in list(active):
                            try:
                                next(g)
                            except StopIteration:
                                active.remove((par, g))
                                free.append(par)

                def p1_load(t_):
                    c0_ = t_ * 512
                    n_ = min(512, N - c0_)
                    DMA(xt[t_ % 2][:, :, 0:n_], x_src.rearrange("(c p) n -> p c n", p=128)[:, :, c0_:c0_ + n_],
                        [tR("x", t_)] if l > 0 else [], [rs("xt%d" % (t_ % 2))])

                p1_load(0)
                for t in range(ntile):
                    c0 = t * 512
                    n = min(512, N - c0)
                    X = xt[t % 2]
                    RX = rs("xt%d" % (t % 2))
                    if t + 1 < ntile:
                        p1_load(t + 1)
                    SQ = hb[1]
                    RSQ = rs("hb1")
                    ACT(SQ[:, :, 0:n], X[:, :, 0:n], AF.Square, [RX], [RSQ])
                    for c in range(8):
                        MM(pbs[0][:, 0:n], ones_b[:], SQ[:, c, 0:n], c == 0, c == 7, [RSQ, rs("ones_b")], [pbR[0]])
                    rstd_from_ss(pbs[0][:, 0:n], rstd[:, 0:n], 1.0 / D, [pbR[0]], [rs("rstd")])
                    H = hb[0]
                    RH = rs("hb0")
                    for c in range(8):
                        tf = tmpf[c % 2]
                        Rtf = rs("tmpf%d" % (c % 2))
                        STT(tf[:, 0:n], X[:, c, 0:n], MOD(1, c), rstd[:, 0:n], ALU.mult, ALU.mult, [RX, Rm, rs("rstd")], [Rtf])
                        ACT(H[:, c, 0:n], tf[:, 0:n], AF.Identity, [Rtf, Rm], [RH], bias=MOD(0, c))

                    def zchunk(col0, m):
                        b = gbank()
                        for k in range(8):
                            MM(pbs[b][0:m, 0:n], winT[:, k, col0:col0 + m], H[:, k, 0:n], k == 0, k == 7, [RW, RH], [pbR[b]])
                        return b

                    UB = b2[0]
                    for j in range(2):
                        b = zchunk(j * 128, 128)
                        ACT(UB[:, j, 0:n], pbs[b][:, 0:n], AF.Copy, [pbR[b]], [rs("b2_0")])
                    DMA(S_["u"].rearrange("(c p) n -> p c n", p=128)[:, :, c0:c0 + n], UB[:, :, 0:n], [rs("b2_0")], [tR("u", t)])
                    CQ = f2[0]
                    for j in range(2):
                        b = zchunk(256 + j * 128, 128)
                        ACT(CQ[:, j, 0:n], pbs[b][:, 0:n], AF.Copy, [pbR[b]], [rs("f2_0")])
                        ACT(b2[2][:, j, 0:n], pbs[b][:, 0:n], AF.Square, [pbR[b]], [rs("b2_2")])
                    for j in range(2):
                        MM(pbs[0][:, 0:n], ones_b[:], b2[2][:, j, 0:n], j == 0, j == 1, [rs("b2_2"), rs("ones_b")], [pbR[0]])
                    rstd_from_ss(pbs[0][:, 0:n], rstd[:, 0:n], 1.0 / 256, [pbR[0]], [rs("rstd")])
                    CQN = b2[1]
                    for j in range(2):
                        STT(CQN[:, j, 0:n], CQ[:, j, 0:n], V("qan", j), rstd[:, 0:n], ALU.mult, ALU.mult,
                            [rs("f2_0"), Rv, rs("rstd")], [rs("b2_1")])
                    b = zchunk(512, 128)
                    ACT(tmpf[0][:, 0:n], pbs[b][:, 0:n], AF.Copy, [pbR[b]], [rs("tmpf0")])
                    ACT(tmpb[1][:, 0:n], pbs[b][:, 0:n], AF.Square, [pbR[b]], [rs("tmpb1")])
                    MM(pbs[0][:, 0:n], ones_b[:], tmpb[1][:, 0:n], True, True, [rs("tmpb1"), rs("ones_b")], [pbR[0]])
                    rstd_from_ss(pbs[0][:, 0:n], rstd[:, 0:n], 1.0 / 128, [pbR[0]], [rs("rstd")])
                    STT(tmpf[1][:, 0:n], tmpf[0][:, 0:n], V("kvan", 0), rstd[:, 0:n], ALU.mult, ALU.mult,
                        [rs("tmpf0"), Rv, rs("rstd")], [rs("tmpf1")])
                    kvR = tR("ckvres", t)
                    ACT(ckvn_res[:, p.kvoff + c0:p.kvoff + c0 + n], tmpf[1][:, 0:n], AF.Copy, [rs("tmpf1")], [kvR])
                    if not p.ctx:
                        DMA(new_ckvT[l, :, c0:c0 + n], tmpf[1][:, 0:n], [rs("tmpf1")], [rs("out_ckv")])
                    b = zchunk(1152, 32)
                    ACT(tmpf[3][0:32, 0:n], pbs[b][0:32, 0:n], AF.Copy, [pbR[b]], [rs("tmpf3")])
                    krR = tR("krres", t)
                    ACT(kr_res[:, p.kvoff + c0:p.kvoff + c0 + n], tmpf[3][0:32, 0:n], AF.Copy, [rs("tmpf3")], [krR])
                    if not p.ctx:
                        DMA(new_krT[l, :, c0:c0 + n], tmpf[3][0:32, 0:n], [rs("tmpf3")], [rs("out_kr")])
                    UF = f2[1]
                    for j in range(2):
                        b = zchunk(640 + j * 128, 128)
                        ACT(UF[:, j, 0:n], pbs[b][:, 0:n], AF.Copy, [pbR[b]], [rs("f2_1")])
                    YC = f2[2]
                    for s4 in range(n // 128):
                        tsl = slice(s4 * 128, (s4 + 1) * 128)
                        pv = pbs[6]
                        for k in range(8):
                            MM(pv[:, 0:256], H[:, k, tsl], winT[:, k, 896:1152], k == 0, k == 7, [RW, RH], [pbR[6]])
                        ACT(tmpb[2][:, 0:256], pv[:, 0:256], AF.Square, [pbR[6]], [rs("tmpb2")])
                        P.op("dve", lambda e: e.tensor_reduce(out=smallc[:, 0:1], in_=tmpb[2][:, 0:256], axis=AX.X, op=ALU.add),
                             reads=[rs("tmpb2")], writes=[rs("smallc")])
                        rstd_from_ss(smallc[:, 0:1], smallc[:, 1:2], 1.0 / 256, [rs("smallc")], [rs("smallc")])
                        pv3 = pv[:, 0:256].rearrange("p (h c) -> p h c", h=4)
                        gv3 = V("gvn", 0, 256).rearrange("p (h c) -> p h c", h=4)
                        for par in range(2):
                            hs = slice(par, 4, 2)
                            STT(vpad[:, hs, par * 64:(par + 1) * 64], pv3[:, hs, :], smallc[:, 1:2], gv3[:, hs, :],
                                ALU.mult, ALU.mult, [pbR[6], rs("smallc"), Rv], [rs("vpad")])
                        for j in range(2):
                            pmx = pbs[7]
                            for hh in (2 * j, 2 * j + 1):
                                MM(pmx[:, 0:128], vpad[:, hh, :], gws[:, hh, :], hh == 2 * j, hh == 2 * j + 1,
                                   [rs("vpad"), rs("wsm")], [pbR[7]])
                            TT("dve", tmpf[0][:, 0:128], pmx[:, 0:128], V("gbias", j * 128, 128), ALU.add, [pbR[7], Rv], [rs("tmpf0")])
                            TT("dve", YC[:, j, tsl], tmpf[0][:, 0:128], UF[:, j, tsl], ALU.mult, [rs("tmpf0"), rs("f2_1")], [rs("f2_2")])
                    for j in range(2):
                        ACT(b2[2][:, j, 0:n], YC[:, j, 0:n], AF.Square, [rs("f2_2")], [rs("b2_2")])
                    for j in range(2):
                        MM(pbs[0][:, 0:n], ones_b[:], b2[2][:, j, 0:n], j == 0, j == 1, [rs("b2_2"), rs("ones_b")], [pbR[0]])
                    rstd_from_ss(pbs[0][:, 0:n], rstd[:, 0:n], 1.0 / 256, [pbR[0]], [rs("rstd")])
                    for j in range(2):
                        STT(b2[2][:, j, 0:n], YC[:, j, 0:n], V("won", 6 + j), rstd[:, 0:n], ALU.mult, ALU.mult,
                            [rs("f2_2"), Rv, rs("rstd")], [rs("b2_2")])
                    DMA(S_["yc"].rearrange("(c p) n -> p c n", p=128)[:, :, c0:c0 + n], b2[2][:, :, 0:n], [rs("b2_2")], [tR("yc", t)])
                    def q_maker(hh, n=n, c0=c0, t=t, CQN=CQN):
                        def mk(par):
                            qo = qtile[par]
                            qR = rs("qtile%d" % par)

                            def raw(psrc, psR):
                                for k in range(2):
                                    MM(psrc, wuq[:, k, hh * 96:(hh + 1) * 96], CQN[:, k, 0:n], k == 0, k == 1, [RW, rs("b2_1")], [psR])

                            def fin():
                                DMA(S_["q"][hh, :, c0:c0 + n], qo[:, 0:n], [qR], [tR("q%d" % hh, t)])
                            return qk_gen(raw, V("qn", 0, 1, 96), qo[:, 0:n], qR, n, c0 if p.ctx else None, par, fin)
                        return mk
                    run_rr([q_maker(hh) for hh in range(8)])

                def ssm_phase():
                    DMA(WB[:, 0:4096], ssmB_b[l], [rs("ssmB_b%d" % l)], [RW])
                    DMA(WB[:, 4096:4096 + 6144], ssmC_b[l], [rs("ssmC_b%d" % l)], [RW])
                    Bp = WB[:, 0:4096].rearrange("p (g r m) -> p g r m", g=16, r=2)
                    Cm = WB[:, 4096:4096 + 6144].rearrange("p (t g m) -> p t g m", t=3, g=16)
                    nch_seq = L // 128
                    Rc = rs("carry")
                    rt1 = xt[0][:].rearrange("p a b -> p (a b)")
                    rt2 = xt[1][:].rearrange("p a b -> p (a b)")
                    Rrt = [rs("xt0"), rs("xt1")]
                    for (rt_, Rr_, sgn, tab) in ((rt1, Rrt[0], 1.0, rr1), (rt2, Rrt[1], -1.0, rr2)):
                        v4 = rt_.rearrange("p (g t two) -> p g t two", g=16, two=2)
                        CP("pool", v4[:, :, :, 0], tab[:, :, 0:1].to_broadcast([128, 16, 128]), [Rs], [Rr_])
                        MS("pool", v4[:, :, :, 1], sgn, [Rr_])
                    if p.ctx:
                        DMA(carry[:], h0in[l], [], [Rc])
                        CP("dve", sc["h0r"][:], carry[:, :, 0], [Rc], [Rs])
                        CP("dve", sc["h0i"][:], carry[:, :, 1], [Rc], [Rs])
                    for d in range(2):
                        cur = dict(tile=None)
                        pending = []
                        cctr = [0]

                        pending2 = []

                        def flush(full=True):
                            nb2 = len(pending2)
                            while pending:
                                pending.pop(0)()
                            for _ in range(nb2 if not full else len(pending2)):
                                pending2.pop(0)()

                        def make_B(gp, gd, half, wk, Rwk, ypsum, yb_, d, tc_, ts_, uidx, tail):
                            def Bfn():
                                wkf = wk[:].rearrange("p a b -> p (a b)")
                                grl, gil = wkf[:, 767:768], wkf[:, 1023:1024]
                                g2 = wkf[:, 512:1024].rearrange("p (r t two) -> p r t two", r=2, two=2)[:, :, :, 1]
                                cTc, sTc, nsTc = sc["cT"][:, gd:gd + 1], sc["sT"][:, gd:gd + 1], sc["nsT"][:, gd:gd + 1]
                                Rcg = rs("carry%d" % gd)
                                c0_ = 2 + 2 * (uidx % 2)
                                Rsm = rs("sca%d" % (uidx % 2))
                                ACT(smallc[:, c0_:c0_ + 1], gil, AF.Identity, [Rwk, Rs], [Rsm], scale=nsTc)
                                ACT(smallc[:, c0_ + 1:c0_ + 2], gil, AF.Identity, [Rwk, Rs], [Rsm], scale=cTc)
                                ACT(carry[:, gd, 0:1], grl, AF.Identity, [Rwk, Rs, Rsm], [Rcg], scale=cTc, bias=smallc[:, c0_:c0_ + 1])
                                ACT(carry[:, gd, 1:2], grl, AF.Identity, [Rwk, Rs, Rsm], [Rcg], scale=sTc, bias=smallc[:, c0_ + 1:c0_ + 2])
                                q4 = sq4[uidx % 3]
                                Rq4 = rs("sq4_%d" % (uidx % 3))
                                tcs = Tcs[:, gd, :, :]
                                if d == 0:
                                    oa, ob_ = q4[:, 0:2, :], q4[:, 2:4, :]
                                else:
                                    oa, ob_ = q4[:, 0:2, ::-1], q4[:, 2:4, ::-1]
                                TT("pool", oa, g2, tcs, ALU.mult, [Rwk, RT], [Rq4])
                                TT("pool", ob_, g2, tcs[:, ::-1, :], ALU.mult, [Rwk, RT], [Rq4])

                                def B2fn():
                                    for i, ti in enumerate((0, 1, 2, 2)):
                                        MM(ypsum[:, half, :], Cm[:, ti, gd, :], q4[:, i, :], (gp % 4 == 0 and i == 0), (gp % 4 == 3 and i == 3),
                                           [RW, Rq4], [yb_])
                                    if tail is not None:
                                        tail()
                                pending2.append(B2fn)
                            return Bfn

                        def make_tail(YT, RYT, UBt, RUB, ypsum, yb_, o0, t, tn, done_tile, d):
                            def tail():
                                if d == 0:
                                    CP("dve", YT[:, :, o0:o0 + 128], ypsum, [yb_], [RYT])
                                else:
                                    TT("dve", YT[:, :, o0:o0 + 128], YT[:, :, o0:o0 + 128], ypsum, ALU.add, [yb_, RYT], [RYT])
                                if not done_tile:
                                    return
                                if d == 0:
                                    DMA(S_["yf"].rearrange("(c p) n -> p c n", p=128)[:, :, t * 512:t * 512 + tn], YT[:, :, 0:tn],
                                        [RYT], [tR("yf", t)])
                                    return
                                for j in range(2):
                                    STT(YT[:, j, 0:tn], UBt[:, j, 0:tn], V("ssmd", j), YT[:, j, 0:tn], ALU.mult, ALU.add,
                                        [RUB, Rv, RYT], [RYT])
                                GF = f2[2]
                                GB = b2[2]
                                ACT(GF[:, :, 0:tn], YT[:, :, 0:tn], AF.Gelu, [RYT], [rs("f2_2")])
                                CP("pool", GB[:, :, 0:tn], GF[:, :, 0:tn], [rs("f2_2")], [rs("b2_2")])
                                for j in range(2):
                                    b = 2
                                    for k in range(2):
                                        MM(pbs[b][:, 0:tn], wglu[:, k, j * 128:(j + 1) * 128], GB[:, k, 0:tn], k == 0, k == 1,
                                           [rs("wsm"), rs("b2_2")], [pbR[b]])
                                    ACT(tmpf[0][:, 0:tn], pbs[b][:, 0:tn], AF.Sigmoid, [pbR[b]], [rs("tmpf0")])
                                    TT("dve", GF[:, j, 0:tn], GF[:, j, 0:tn], tmpf[0][:, 0:tn], ALU.mult, [rs("f2_2"), rs("tmpf0")], [rs("f2_2")])
                                for j in range(2):
                                    ACT(GB[:, j, 0:tn], GF[:, j, 0:tn], AF.Square, [rs("f2_2")], [rs("b2_2")])
                                for j in range(2):
                                    MM(pbs[2][:, 0:tn], ones_b[:], GB[:, j, 0:tn], j == 0, j == 1, [rs("b2_2"), rs("ones_b")], [pbR[2]])
                                rstd_from_ss(pbs[2][:, 0:tn], rstd[:, 0:tn], 1.0 / 256, [pbR[2]], [rs("rstd")])
                                for j in range(2):
                                    STT(GB[:, j, 0:tn], GF[:, j, 0:tn], V("won", j), rstd[:, 0:tn], ALU.mult, ALU.mult,
                                        [rs("f2_2"), Rv, rs("rstd")], [rs("b2_2")])
                                DMA(S_["ya"].rearrange("(c p) n -> p c n", p=128)[:, :, t * 512:t * 512 + tn], GB[:, :, 0:tn],
                                    [rs("b2_2")], [tR("ya", t)])
                            return tail

                        uctr = [0]
                        for s in range(p.nseq):
                            flush()
                            if p.ctx:
                                TT("dve", S("t3"), S("fir"), S("h0r"), ALU.mult, [Rs], [Rs])
                                TT("dve", S("t4"), S("fii"), S("h0i"), ALU.mult, [Rs], [Rs])
                                TT("dve", carry[:, d:16:2, 0], S("t3")[:, d:16:2], S("t4")[:, d:16:2], ALU.subtract, [Rs], [Rc])
                                TT("dve", S("t3"), S("fir"), S("h0i"), ALU.mult, [Rs], [Rs])
                                TT("dve", S("t4"), S("fii"), S("h0r"), ALU.mult, [Rs], [Rs])
                                TT("dve", carry[:, d:16:2, 1], S("t3")[:, d:16:2], S("t4")[:, d:16:2], ALU.add, [Rs], [Rc])
                            else:
                                MS("dve", carry[:, d:16:2, :], 0.0, [Rc])
                            chunks = list(range(nch_seq))
                            if d == 1:
                                chunks = chunks[::-1]
                            for ci, ch in enumerate(chunks):
                                g0 = s * L + ch * 128
                                t = g0 // 512
                                o0 = g0 - t * 512
                                tn = min(512, N - t * 512)
                                if cur["tile"] != t:
                                    cur["tile"] = t
                                    cur["UBt"] = b2[(t + d) % 2]
                                    cur["RUB"] = rs("b2_%d" % ((t + d) % 2))
                                    DMA(cur["UBt"][:, :, 0:tn], S_["u"].rearrange("(c p) n -> p c n", p=128)[:, :, t * 512:t * 512 + tn],
                                        [tR("u", t)], [cur["RUB"]])
                                    cur["YT"] = f2[(t + d) % 2]
                                    cur["RYT"] = rs("f2_%d" % ((t + d) % 2))
                                    if d == 1:
                                        DMA(cur["YT"][:, :, 0:tn], S_["yf"].rearrange("(c p) n -> p c n", p=128)[:, :, t * 512:t * 512 + tn],
                                            [tR("yf", t)], [cur["RYT"]])
                                UBt, RUB, YT, RYT = cur["UBt"], cur["RUB"], cur["YT"], cur["RYT"]
                                cctr[0] += 1
                                yb_ = pbR[2]
                                ypsum = pbs[2][:, 0:256].rearrange("p (h n) -> p h n", h=2)
                                last_in_tile = (ci == len(chunks) - 1) or ((s * L + chunks[ci + 1] * 128) // 512 != t)
                                done_tile = last_in_tile and not any(
                                    ((s2 * L) // 512 == t or ((s2 + 1) * L - 1) // 512 == t) for s2 in range(s + 1, p.nseq))
                                for gp in range(8):
                                    gd = gp * 2 + d
                                    half = gp // 4
                                    uidx = uctr[0]
                                    uctr[0] += 1
                                    wk = sw[uidx % 2]
                                    Rwk = rs("sw%d" % (uidx % 2))
                                    hb_ = uidx % 2
                                    Rpb = pbR[hb_]
                                    pb2 = pbs[hb_][:, 0:256].rearrange("p (r n) -> p r n", r=2)
                                    for r_ in range(2):
                                        MM(pb2[:, r_, :], Bp[:, gd, r_, :], UBt[:, half, o0:o0 + 128], True, True, [RW, RUB], [Rpb])
                                    if d == 0:
                                        pbr, pbi = pb2[:, 0, :], pb2[:, 1, :]
                                    else:
                                        pbr, pbi = pb2[:, 0, ::-1], pb2[:, 1, ::-1]
                                    tcs = Tcs[:, gd, :, :]
                                    tc_, ts_ = None, None
                                    if d == 0:
                                        pa, pbsw = pb2, pb2[:, ::-1, :]
                                    else:
                                        pa, pbsw = pb2[:, :, ::-1], pb2[:, ::-1, ::-1]
                                    wkf = wk[:].rearrange("p a b -> p (a b)")
                                    iv = lambda off: wkf[:, off:off + 256].rearrange("p (t two) -> p two t", two=2)
                                    TT("dve", iv(0), pa, tcs, ALU.mult, [Rpb, RT], [Rwk])
                                    TT("dve", iv(256), pa, tcs[:, ::-1, :], ALU.mult, [Rpb, RT], [Rwk])
                                    Rcg = rs("carry%d" % gd)
                                    for (off, dst, ri, rt_, Rr_) in ((0, 512, 0, rt1, Rrt[0]), (256, 768, 1, rt2, Rrt[1])):
                                        P.op("dve", lambda e, off=off, dst=dst, ri=ri, wkf=wkf, rt_=rt_, gd=gd: e.tensor_tensor_scan(
                                            out=wkf[:, dst:dst + 256], data0=rt_[:, gd * 256:(gd + 1) * 256], data1=wkf[:, off:off + 256],
                                            initial=carry[:, gd, ri:ri + 1], op0=ALU.mult, op1=ALU.add),
                                            reads=[Rwk, Rr_, Rc, Rcg], writes=[Rwk])
                                    tail = None
                                    if gp == 7:
                                        tail = make_tail(YT, RYT, UBt, RUB, ypsum, yb_, o0, t, tn, done_tile, d)
                                    Bc = make_B(gp, gd, half, wk, Rwk, ypsum, yb_, d, tc_, ts_, uidx, tail)
                                    flush(full=False)
                                    pending.append(Bc)
                                    yield
                            flush()
                            if not p.ctx:
                                Rcs = [Rc] + [rs("carry%d" % (gp * 2 + d)) for gp in range(8)]
                                cr_, ci_ = carry[:, d:16:2, 0], carry[:, d:16:2, 1]
                                fo_r, fo_i = sc["for"][:, d:16:2], sc["foi"][:, d:16:2]
                                a_, b_ = sc["t3"][:, 0:8], sc["t4"][:, 0:8]
                                Rh = rs("hfin")
                                TT("dve", a_, fo_r, cr_, ALU.mult, [Rs] + Rcs, [Rs])
                                TT("dve", b_, fo_i, ci_, ALU.mult, [Rs] + Rcs, [Rs])
                                TT("dve", hfin[:, d:16:2, 0], a_, b_, ALU.subtract, [Rs], [Rh])
                                TT("dve", a_, fo_r, ci_, ALU.mult, [Rs] + Rcs, [Rs])
                                TT("dve", b_, fo_i, cr_, ALU.mult, [Rs] + Rcs, [Rs])
                                TT("dve", hfin[:, d:16:2, 1], a_, b_, ALU.add, [Rs], [Rh])
                                DMA(new_h[l, s][:, d:16:2, :], hfin[:, d:16:2, :], [Rh], [rs("out_h")])


                def attn_phase():
                    RU = rs("U1")
                    Kh = U1[0:96, 0:p.NKV]
                    nkt = p.NKV // 128
                    Vh = U1[:, 4352:4352 + nkt * 65].rearrange("p (t c) -> p t c", c=65)
                    fin_pending = []
                    fin2_pending = []
                    for hh in range(8):
                        blocks = []
                        if p.ctx:
                            blocks.append((0, PAST, None, [rs("ckvres_c"), rs("krres_c")]))
                        for t in range(ntile):
                            n = min(512, N - t * 512)
                            blocks.append((p.kvoff + t * 512, n, (t * 512) if p.ctx else None, [tR("ckvres", t), tR("krres", t)]))
                        def k_maker(k0, kn, rpos, rr, hh=hh):
                            def mk(par):
                                RK = rs("Kp%d" % par)

                                def raw(psrc, psR):
                                    MM(psrc, wukvk[:, hh, :], ckvn_res[:, k0:k0 + kn], True, False, [rs("wsm")] + rr, [psR])
                                    MM(psrc, id3296, kr_res[:, k0:k0 + kn], False, True, [rs("cbf")] + rr, [psR])

                                def fin():
                                    for k4 in range(0, kn // 128, 4):
                                        nk4 = min(4, kn // 128 - k4)
                                        pv = pbs[7][:, 0:256].rearrange("p (t c) -> p t c", c=64)
                                        for i in range(nk4):
                                            kk = k0 + (k4 + i) * 128
                                            MM(pv[:, i, :], ckvn_res[:, kk:kk + 128], wukvv[:, hh * 64:(hh + 1) * 64], True, True,
                                               [rs("wsm")] + rr, [pbR[7]])
                                        kt0 = k0 // 128 + k4
                                        CP("dve", Vh[:, kt0:kt0 + nk4, 0:64], pv[:, 0:nk4, :], [pbR[7]], [RU])
                                return qk_gen(raw, V("kn", 0, 1, 96), Kh[:, k0:k0 + kn], RK, kn, rpos, par, fin, extraW=[RU], banks=((3, 4), (5, 6)))
                            return mk
                        yield from run_rr_gen([k_maker(*blk) for blk in blocks])
                        MS("pool", Vh[:, :, 64:65], 1.0, [RU])
                        for s in range(p.nseq):
                            if p.ctx:
                                ktl = list(range(nkt))
                            else:
                                ktl = list(range(s * L // 128, (s + 1) * L // 128))
                            for q0 in range(s * L, (s + 1) * L, 512):
                                nq = min(512, (s + 1) * L - q0)
                                t = q0 // 512
                                qctr[0] += 1
                                qi = qctr[0] % 2
                                QT = qtile[qi]
                                RQ = rs("qtile%d" % qi)
                                DMA(QT[:, 0:nq], S_["q"][hh, :, q0:q0 + nq], [tR("q%d" % hh, t)], [RQ])
                                pob = 6 + (qctr[0] % 2)
                                po = pbs[pob]
                                pendq = []
                                nkl = len(ktl)
                                sbank = {}

                                def score(i):
                                    b = 3 + (actr[0] % 3)
                                    actr[0] += 1
                                    sbank[i] = b
                                    MM(pbs[b][:, 0:nq], Kh[:, ktl[i] * 128:(ktl[i] + 1) * 128], QT[:, 0:nq], True, True, [RU, rs("Kp0"), rs("Kp1"), RQ], [pbR[b]])

                                score(0)
                                if nkl > 1:
                                    score(1)
                                for i, kt in enumerate(ktl):
                                    if i + 2 < nkl:
                                        score(i + 2)
                                    b = sbank[i]
                                    pt = ptile[i % 3]
                                    Rpt = rs("ptile%d" % (i % 3))
                                    ACT(pt[:, 0:nq], pbs[b][:, 0:nq], AF.Exp, [pbR[b]], [Rpt], scale=ATTN_SCALE)
                                    if i >= 1:
                                        pi, pkt, ppt, pRpt = pendq.pop(0)
                                        MM(po[0:65, 0:nq], Vh[:, pkt, :], ppt[:, 0:nq], pi == 0, False, [RU, pRpt], [pbR[pob]])
                                    if i == min(4, nkl - 1) and fin_pending:
                                        fin2_pending.append(fin_pending.pop(0)())
                                    if i == min(24, nkl - 1) and fin2_pending:
                                        fin2_pending.pop(0)()
                                    pendq.append((i, kt, pt, Rpt))
                                    yield
                                pi, pkt, ppt, pRpt = pendq.pop(0)
                                MM(po[0:65, 0:nq], Vh[:, pkt, :], ppt[:, 0:nq], pi == 0, True, [RU, pRpt], [pbR[pob]])

                                def make_fin(po, pob, nq, hh, q0, t):
                                    def fin():
                                        CP("dve", osb[:, 0:nq], po[0:65, 0:nq], [pbR[pob]], [rs("osb")])
                                        RECIP(osb[64:65, 0:nq], osb[64:65, 0:nq], [rs("osb")], [rs("osb")])

                                        def fin2():
                                            bc = pob
                                            MM(pbs[bc][0:64, 0:nq], ones_f[64:65, 0:64], osb[64:65, 0:nq], True, True, [rs("osb"), rs("ones_f")], [pbR[bc]])
                                            ob = tmpb[3]
                                            TT("dve", ob[0:64, 0:nq], osb[0:64, 0:nq], pbs[bc][0:64, 0:nq], ALU.mult, [rs("osb"), pbR[bc]], [rs("tmpb3")])
                                            DMA(S_["yb"][hh * 64:(hh + 1) * 64, q0:q0 + nq], ob[0:64, 0:nq], [rs("tmpb3")], [tR("yb%d" % hh, t)])
                                        return fin2
                                    return fin
                                fin_pending.append(make_fin(po, pob, nq, hh, q0, t))
                        while fin_pending:
                            fin2_pending.append(fin_pending.pop(0)())
                        while fin2_pending:
                            fin2_pending.pop(0)()


                n_ssm = max(1, N // 8)
                n_att = 8 * (sum(len(range(s_ * L, (s_ + 1) * L, 512)) * ((p.NKV // 128) if p.ctx else (L // 128)) for s_ in range(p.nseq)) + 8 * (ntile + 1))
                import os as _os
                if _os.environ.get("MK_DRIVE", "il") == "seq":
                    for _ in ssm_phase():
                        pass
                    for _ in attn_phase():
                        pass
                else:
                    drive(ssm_phase(), attn_phase(), max(1, int(round(n_att / n_ssm))))

                if l == 0:
                    for s_ in range(p.nseq):
                        for col in (0, p.L + 1):
                            DMA(S_["h2"].rearrange("(c p) s n -> p c s n", p=128)[:, :, s_, col:col + 1], zeros_b[:, 0:8].rearrange("p (c o) -> p c o", o=1),
                                [rs("zeros_b")], [rs("h2_%s_%d" % (pn, s_))])
                DMA(WB[:, 0:8192], w_out_b[l], [rs("w_out_b%d" % l)], [RW])
                woT = WB[:, 0:8192].rearrange("p (k m) -> p k m", k=8)
                for t in range(ntile):
                    c0 = t * 512
                    n = min(512, N - c0)
                    YC8 = hb[1]
                    RY8 = rs("hb1")
                    DMA(YC8[:, 0:2, 0:n], S_["ya"].rearrange("(c p) n -> p c n", p=128)[:, :, c0:c0 + n], [tR("ya", t)], [RY8])
                    DMA(YC8[:, 2:6, 0:n], S_["yb"].rearrange("(c p) n -> p c n", p=128)[:, :, c0:c0 + n],
                        [tR("yb%d" % hh, t) for hh in range(8)], [RY8])
                    DMA(YC8[:, 6:8, 0:n], S_["yc"].rearrange("(c p) n -> p c n", p=128)[:, :, c0:c0 + n], [tR("yc", t)], [RY8])
                    X = xt[0]
                    RX = rs("xt0")
                    DMA(X[:, :, 0:n], x_src.rearrange("(c p) n -> p c n", p=128)[:, :, c0:c0 + n], [tR("x", t)] if l > 0 else [], [RX])
                    SQ = hb[0]
                    RSQ = rs("hb0")
                    ACT(SQ[:, 0:4, 0:n], YC8[:, 2:6, 0:n], AF.Square, [RY8], [RSQ])
                    for j in range(4):
                        MM(pbs[0][:, 0:n], ones_b[:], SQ[:, j, 0:n], j == 0, j == 3, [RSQ, rs("ones_b")], [pbR[0]])
                    rstd_from_ss(pbs[0][:, 0:n], rstd[:, 0:n], 1.0 / 512, [pbR[0]], [rs("rstd")])
                    for j in range(4):
                        STT(YC8[:, 2 + j, 0:n], YC8[:, 2 + j, 0:n], V("won", 2 + j), rstd[:, 0:n], ALU.mult, ALU.mult,
                            [RY8, Rv, rs("rstd")], [RY8])
                    X1 = xt[1]
                    RX1 = rs("xt1")
                    for o in range(8):
                        b = gbank()
                        for k in range(8):
                            MM(pbs[b][:, 0:n], woT[:, k, o * 128:(o + 1) * 128], YC8[:, k, 0:n], k == 0, k == 7, [RW, RY8], [pbR[b]])
                        STT(X1[:, o, 0:n], pbs[b][:, 0:n], MOD(2, o), X[:, o, 0:n], ALU.mult, ALU.add, [pbR[b], Rm, RX], [RX1])
                    DMA(S_["xa"].rearrange("(c p) n -> p c n", p=128)[:, :, c0:c0 + n], X1[:, :, 0:n], [RX1], [tR("xa", t)])
                    ACT(SQ[:, :, 0:n], X1[:, :, 0:n], AF.Square, [RX1], [RSQ])
                    for c in range(8):
                        MM(pbs[0][:, 0:n], ones_b[:], SQ[:, c, 0:n], c == 0, c == 7, [RSQ, rs("ones_b")], [pbR[0]])
                    rstd_from_ss(pbs[0][:, 0:n], rstd[:, 0:n], 1.0 / D, [pbR[0]], [rs("rstd")])
                    for c in range(8):
                        tf = tmpf[c % 2]
                        Rtf = rs("tmpf%d" % (c % 2))
                        STT(tf[:, 0:n], X1[:, c, 0:n], MOD(4, c), rstd[:, 0:n], ALU.mult, ALU.mult, [RX1, Rm, rs("rstd")], [Rtf])
                        ACT(SQ[:, c, 0:n], tf[:, 0:n], AF.Identity, [Rtf, Rm], [RSQ], bias=MOD(3, c))
                    for s in range(p.nseq):
                        lo, hi = max(c0, s * L), min(c0 + n, (s + 1) * L)
                        if lo < hi:
                            DMA(S_["h2"].rearrange("(c p) s n -> p c s n", p=128)[:, :, s, 1 + lo - s * L:1 + hi - s * L],
                                SQ[:, :, lo - c0:hi - c0], [RSQ], [rs("h2_%s_%d" % (pn, s))])

                ACTB = U1[:, 0:NF * 512].rearrange("p (f n) -> p f n", f=NF)
                RU = rs("U1")
                ftile = 0
                ftiles = [(s, j0) for s in range(p.nseq) for j0 in range(0, L, 510)]

                def ffn_load(fi):
                    s, j0 = ftiles[fi]
                    no = min(510, L - j0)
                    ncol = no + 2
                    g0 = s * L + j0
                    tl = sorted(set([g0 // 512, (g0 + no - 1) // 512]))
                    DMA(hb[fi % 2][:, :, 0:ncol], S_["h2"].rearrange("(c p) s n -> p c s n", p=128)[:, :, s, j0:j0 + ncol],
                        [rs("h2_%s_%d" % (pn, s))], [rs("hb%d" % (fi % 2))])
                    DMA(xt[fi % 2][:, :, 0:no], S_["xa"].rearrange("(c p) n -> p c n", p=128)[:, :, g0:g0 + no],
                        [tR("xa", t) for t in tl], [rs("xt%d" % (fi % 2))])

                ffn_load(0)
                for (s, j0) in ftiles:
                    if True:
                        no = min(510, L - j0)
                        ncol = no + 2
                        H2 = hb[ftile % 2]
                        RH2 = rs("hb%d" % (ftile % 2))
                        g0 = s * L + j0
                        X1 = xt[ftile % 2]
                        RX1 = rs("xt%d" % (ftile % 2))
                        tl = sorted(set([g0 // 512, (g0 + no - 1) // 512]))
                        for f in range(NF):
                            ri = f % 3
                            rb = ring_up[ri]
                            rR = rs("rup%d" % ri)
                            DMA(rb[:], w_up_b[l, f], [rs("w_up_b%d" % l)], [rR])
                            rbv = rb[:].rearrange("p (k m) -> p k m", k=8)
                            bg = 1 + (f % 2)
                            bv = 3 + (f % 2)
                            for k in range(8):
                                MM(pbs[bg][:, 0:ncol], rbv[:, k, 0:128], H2[:, k, 0:ncol], k == 0, k == 7, [rR, RH2], [pbR[bg]])
                            for k in range(8):
                                MM(pbs[bv][:, 0:ncol], rbv[:, k, 128:256], H2[:, k, 0:ncol], k == 0, k == 7, [rR, RH2], [pbR[bv]])
                            outs = []
                            for (bb, chn, tfi) in ((bg, f, 0), (bv, NF + f, 1)):
                                tf = tmpf[tfi]
                                Rtf = rs("tmpf%d" % tfi)
                                ps_ = pbs[bb]
                                ACT(tf[:, 0:no], ps_[:, 1:1 + no], AF.Identity, [pbR[bb], Rv], [Rtf],
                                    scale=V("convw", 44 + chn), bias=V("convb", chn))
                                STT(tf[:, 0:no], ps_[:, 0:no], V("convw", chn), tf[:, 0:no], ALU.mult, ALU.add, [pbR[bb], Rv, Rtf], [Rtf])
                                STT(tf[:, 0:no], ps_[:, 2:2 + no], V("convw", 88 + chn), tf[:, 0:no], ALU.mult, ALU.add,
                                    [pbR[bb], Rv, Rtf], [Rtf])
                            ACT(tmpf[2][:, 0:no], tmpf[0][:, 0:no], AF.Silu, [rs("tmpf0")], [rs("tmpf2")])
                            TT("pool", ACTB[:, f, 0:no], tmpf[2][:, 0:no], tmpf[1][:, 0:no], ALU.mult, [rs("tmpf2"), rs("tmpf1")], [RU])
                        if ftile + 1 < len(ftiles):
                            ffn_load(ftile + 1)
                        for o in range(8):
                            b = 5 + (o % 2)
                            for hf in range(2):
                                ri = (2 * o + hf) % 3
                                rb = ring_up[ri]
                                rR = rs("rup%d" % ri)
                                DMA(rb[:, 0:11 * 128], w_down_b[l, o][:, hf * 1408:(hf + 1) * 1408], [rs("w_down_b%d" % l)], [rR])
                                rbv = rb[:, 0:11 * 128].rearrange("p (f m) -> p f m", f=11)
                                for fi in range(11):
                                    f = hf * 11 + fi
                                    MM(pbs[b][:, 0:no], rbv[:, fi, :], ACTB[:, f, 0:no], f == 0, f == NF - 1, [rR, RU], [pbR[b]])
                            STT(X1[:, o, 0:no], pbs[b][:, 0:no], MOD(5, o), X1[:, o, 0:no], ALU.mult, ALU.add, [pbR[b], Rm, RX1], [RX1])
                        DMA(x_dst.rearrange("(c p) n -> p c n", p=128)[:, :, g0:g0 + no], X1[:, :, 0:no], [RX1],
                            [tR("x", t) for t in tl] + [rs("xout_" + pn)])
                        ftile += 1

        P.op("sp", lambda e: e.nop(), reads=[rs("xout_p"), rs("xout_s"), rs("out_ckv"), rs("out_kr"), rs("out_h")])
        P.emit(st)
        build_program.stats = P.stats
    return nc


def _f(a):
    return np.ascontiguousarray(np.asarray(a, dtype=np.float32))


def _chunkcols(v):
    v = np.asarray(v, np.float32)
    return np.ascontiguousarray(v.reshape(-1, 128).T)


def _ktile(w):
    K, M = w.shape
    return np.ascontiguousarray(w.reshape(K // 128, 128, M).transpose(1, 0, 2).reshape(128, -1))


def _rope_tables(L):
    rows = L // GRID_W
    row = np.repeat(np.arange(rows, dtype=np.float32), GRID_W)
    col = np.tile(np.arange(GRID_W, dtype=np.float32), rows)
    n_freq = 8
    inv = (np.float32(10000.0) ** (-np.arange(n_freq, dtype=np.float32) / np.float32(n_freq))).astype(np.float32)
    ang = np.concatenate([row[:, None] * inv, col[:, None] * inv], axis=-1).astype(np.float32)
    c, s = np.cos(ang).astype(np.float32), np.sin(ang).astype(np.float32)
    C = np.concatenate([c.T, c.T], axis=0)
    S = np.concatenate([s.T, s.T], axis=0)
    return _f(C), _f(S)


def _shared_inputs(I, dec_seq):
    out = {}
    ropeC, ropeS = _rope_tables(dec_seq)
    out["ropeC"], out["ropeS"] = ropeC, ropeS
    consts = np.zeros((128, 256), np.float32)
    for m in range(16):
        consts[m + 16, m] = -1.0
    for m in range(16, 32):
        consts[m - 16, m] = 1.0
    for k in range(32):
        consts[k, 32 + k] = 1.0
    out["consts"] = consts
    w_in = _f(I["w_in"])
    perm = np.concatenate([np.arange(0, 640), np.arange(672, 928), np.arange(928, 1184), np.arange(640, 672)])
    out["w_in_t"] = np.stack([_ktile(w_in[l][:, perm]) for l in range(DEPTH)])
    out["w_out_t"] = np.stack([_ktile(_f(I["w_out"])[l]) for l in range(DEPTH)])
    hperm = np.concatenate([np.arange(64, 96), np.arange(0, 64)])
    qperm = np.concatenate([h * 96 + hperm for h in range(8)])
    out["w_uq_t"] = np.stack([_ktile(_f(I["w_uq"])[l][:, qperm]) for l in range(DEPTH)])
    w_ukv = _f(I["w_ukv"])
    wk = np.zeros((DEPTH, 128, 8, 96), np.float32)
    wv = np.zeros((DEPTH, 128, 512), np.float32)
    for h in range(8):
        wk[:, :, h, 32:96] = w_ukv[:, :, h * 128:h * 128 + 64]
        wv[:, :, h * 64:(h + 1) * 64] = w_ukv[:, :, h * 128 + 64:h * 128 + 128]
    out["w_ukvk_t"] = wk.reshape(DEPTH, 128, 768)
    out["w_ukvv_t"] = wv
    out["w_glu_t"] = np.stack([_ktile(_f(I["ssm_w_glu"])[l]) for l in range(DEPTH)])
    w_mod = _f(I["w_mod"])
    out["w_mod_t"] = np.ascontiguousarray(
        w_mod.reshape(DEPTH, 8, 128, 24, 256).transpose(0, 3, 2, 1, 4).reshape(DEPTH, 24, 128, 2048))
    w_up = _f(I["ffn_w_up"])
    g = w_up[:, :, :DFF].reshape(DEPTH, 8, 128, NF, 128)
    v = w_up[:, :, DFF:].reshape(DEPTH, 8, 128, NF, 128)
    gv = np.stack([g, v], axis=4)
    out["w_up_t"] = np.ascontiguousarray(gv.transpose(0, 3, 2, 1, 4, 5).reshape(DEPTH, NF, 128, 2048))
    w_dn = _f(I["ffn_w_down"])
    out["w_down_t"] = np.ascontiguousarray(
        w_dn.reshape(DEPTH, NF, 128, 8, 128).transpose(0, 3, 2, 1, 4).reshape(DEPTH, 8, 128, NF * 128))
    Bre, Bim = _f(I["ssm_b_re"]), _f(I["ssm_b_im"])
    Cre, Cim = _f(I["ssm_c_re"]), _f(I["ssm_c_im"])
    Bp = np.zeros((DEPTH, 128, 16, 2, 128), np.float32)
    Cp = np.zeros((DEPTH, 128, 16, 2, 128), np.float32)
    for gp in range(8):
        for d in range(2):
            gd = gp * 2 + d
            for glo in range(2):
                gg = 2 * gp + glo
                k0 = 16 * (gg % 8)
                for r, (B_, C_) in enumerate(((Bre, Cre), (Bim, Cim))):
                    Bp[:, k0:k0 + 16, gd, r, glo * 64:(glo + 1) * 64] = B_[:, d, gg].transpose(0, 2, 1)
                    Cp[:, glo * 64:(glo + 1) * 64, gd, r, k0:k0 + 16] = C_[:, d, gg].transpose(0, 2, 1)
    out["ssmB_t"] = Bp.reshape(DEPTH, 128, 4096)
    out["ssmC_t"] = Cp.reshape(DEPTH, 128, 4096)
    gw = _f(I["gmlp_w_s"])
    out["gmlp_wsT"] = np.ascontiguousarray(gw.transpose(0, 3, 1, 2).reshape(DEPTH, 128, 512))
    vecs = np.zeros((DEPTH, 128, NV), np.float32)

    def put(name, l, arr):
        arr = np.asarray(arr, np.float32)
        vecs[l, :arr.shape[0], VOFF[name]:VOFF[name] + arr.shape[1]] = arr

    for l in range(DEPTH):
        put("bmod", l, _chunkcols(I["b_mod"][l]))
        put("ssmd", l, _chunkcols(I["ssm_d"][l]))
        put("qan", l, _chunkcols(I["q_a_norm"][l]))
        put("kvan", l, _chunkcols(I["kv_a_norm"][l]))
        put("qn", l, _f(I["q_norm"])[l][hperm][:, None])
        put("kn", l, _f(I["k_norm"])[l][hperm][:, None])
        put("won", l, _chunkcols(I["w_out_norm"][l]))
        cw = _f(I["ffn_conv_w"])[l]
        put("convw", l, np.concatenate([_chunkcols(cw[t]) for t in range(3)], axis=1))
        put("convb", l, _chunkcols(I["ffn_conv_b"][l]))
        bs = _f(I["gmlp_b_s"])[l]
        gb = np.zeros((128, 2, 128), np.float32)
        for j in range(2):
            gb[0:64, j, :] = bs[2 * j][None, :]
            gb[64:128, j, :] = bs[2 * j + 1][None, :]
        put("gbias", l, gb.reshape(128, 256))
        put("gvn", l, np.broadcast_to(_f(I["gmlp_v_norm"])[l][None, :], (128, 256)))
        for nm, key in (("are", "ssm_a_re"), ("aim", "ssm_a_im")):
            a = _f(I[key])[l]
            t = np.zeros((128, 16), np.float32)
            for gp in range(8):
                for d in range(2):
                    for glo in range(2):
                        t[glo * 64:(glo + 1) * 64, gp * 2 + d] = a[d, 2 * gp + glo]
            put(nm, l, t)
        ld = _f(I["ssm_log_dt"])[l]
        t = np.zeros((128, 16), np.float32)
        for gp in range(8):
            for d in range(2):
                for glo in range(2):
                    t[glo * 64:(glo + 1) * 64, gp * 2 + d] = ld[d, 2 * gp + glo]
        put("ldt", l, t)
    out["vecs"] = vecs
    return out


_PROG_CACHE = {}


def kernel(x_prompt, x_sample, cache_ckv, cache_krope, state_ssm_re, state_ssm_im, c, c_ctx, **W):
    I = dict(W)
    dec_seq = x_sample.shape[1]
    if dec_seq not in _PROG_CACHE:
        _PROG_CACHE[dec_seq] = build_program(dec_seq)
    nc = _PROG_CACHE[dec_seq]
    shared = _shared_inputs(I, dec_seq)
    xp = _f(x_prompt)
    xs = _f(x_sample)
    B = xp.shape[0]
    n_s = xs.shape[0]
    in_maps = []
    for core in range(N_CORES):
        m = dict(shared)
        seqs = xp[core * NSEQ_P:(core + 1) * NSEQ_P]
        m["xT_p"] = np.ascontiguousarray(seqs.reshape(NSEQ_P * SEQ, D).T)
        cond = np.zeros((D, 2), np.float32)
        cond[:, 0] = _f(c_ctx)
        if core < n_s:
            m["xT_s"] = np.ascontiguousarray(xs[core].T)
            cond[:, 1] = _f(c)[core]
            m["cache_ckvT"] = np.ascontiguousarray(_f(cache_ckv)[core].transpose(0, 2, 1))
            m["cache_krT"] = np.ascontiguousarray(_f(cache_krope)[core].transpose(0, 2, 1))
            h0 = np.zeros((DEPTH, 128, 16, 2), np.float32)
            for r, st_ in enumerate((_f(state_ssm_re)[core], _f(state_ssm_im)[core])):
                for gp in range(8):
                    for d in range(2):
                        for glo in range(2):
                            h0[:, glo * 64:(glo + 1) * 64, gp * 2 + d, r] = st_[:, d, 2 * gp + glo]
            m["h0"] = h0
        else:
            m["xT_s"] = np.zeros((D, dec_seq), np.float32)
            m["cache_ckvT"] = np.zeros((DEPTH, 128, PAST), np.float32)
            m["cache_krT"] = np.zeros((DEPTH, 32, PAST), np.float32)
            m["h0"] = np.zeros((DEPTH, 128, 16, 2), np.float32)
        m["condT"] = np.ascontiguousarray(cond.reshape(8, 128, 2).transpose(1, 0, 2))
        in_maps.append(m)
    res = run_bass_kernel_spmd(nc, in_maps, core_ids=list(range(N_CORES)))
    R_ = res.results
    y_prompt = np.zeros((B, SEQ, D), np.float32)
    y_sample = np.zeros((n_s, dec_seq, D), np.float32)
    new_ckv = np.zeros((B, DEPTH, SEQ, 128), np.float32)
    new_kr = np.zeros((B, DEPTH, SEQ, 32), np.float32)
    new_re = np.zeros((B, DEPTH, 2, 16, 64), np.float32)
    new_im = np.zeros((B, DEPTH, 2, 16, 64), np.float32)
    for core in range(N_CORES):
        r = R_[core]
        y_prompt[core * NSEQ_P:(core + 1) * NSEQ_P] = r["yT_p"].T.reshape(NSEQ_P, SEQ, D)
        if core < n_s:
            y_sample[core] = r["yT_s"].T
        ck = r["new_ckvT"]
        kr = r["new_krT"]
        nh = r["new_h"]
        for s in range(NSEQ_P):
            b = core * NSEQ_P + s
            new_ckv[b] = ck[:, :, s * SEQ:(s + 1) * SEQ].transpose(0, 2, 1)
            new_kr[b] = kr[:, :, s * SEQ:(s + 1) * SEQ].transpose(0, 2, 1)
            for gp in range(8):
                for d in range(2):
                    for glo in range(2):
                        new_re[b, :, d, 2 * gp + glo] = nh[:, s, glo * 64:(glo + 1) * 64, gp * 2 + d, 0]
                        new_im[b, :, d, 2 * gp + glo] = nh[:, s, glo * 64:(glo + 1) * 64, gp * 2 + d, 1]
    return (y_prompt, y_sample, new_ckv, new_kr, new_re, new_im)
```

```python
import math
from contextlib import ExitStack
import numpy as np
import concourse.bass as bass
import concourse.mybir as mybir
from concourse.bass_utils import run_bass_kernel_spmd

F32 = mybir.dt.float32
BF16 = mybir.dt.bfloat16
AF = mybir.ActivationFunctionType
ALU = mybir.AluOpType
AX = mybir.AxisListType

D = 1024
DEPTH = 2
SEQ = 256
NSEQ_P = 2
DEC_SEQ = 4096
PAST = 256
GRID_W = 64
EPS = 1e-6
DFF = 2816
NF = DFF // 128
ATTN_SCALE = 1.0 / math.sqrt(96.0)
N_CORES = 8

SAME_ENGINE_SYNC = True
N_DMA_SEMS = {"sp": 56, "pool": 24}


class Res:
    __slots__ = ("name", "last_w", "readers")

    def __init__(self, name=""):
        self.name = name
        self.last_w = None
        self.readers = []


class Op:
    __slots__ = ("eng", "fn", "deps", "dma", "signal", "ticket", "idx", "prewait")

    def __init__(self, eng, fn, dma):
        self.eng = eng
        self.fn = fn
        self.dma = dma
        self.deps = set()
        self.signal = False
        self.ticket = None
        self.prewait = None


class Prog:
    ENGS = ("pe", "act", "dve", "pool", "sp")

    def __init__(self, nc):
        self.nc = nc
        self.ops = []

    def op(self, eng, fn, reads=(), writes=(), dma=False):
        o = Op(eng, fn, dma)
        o.idx = len(self.ops)
        deps = set()
        for r in reads:
            if r.last_w is not None:
                deps.add(r.last_w)
        for w in writes:
            if w.last_w is not None:
                deps.add(w.last_w)
            deps.update(w.readers)
        for d in deps:
            od = self.ops[d]
            if od.eng == eng and not od.dma:
                if eng == "pe" or not SAME_ENGINE_SYNC:
                    continue
            o.deps.add(d)
            od.signal = True
        for r in reads:
            r.readers.append(o.idx)
        for w in writes:
            w.last_w = o.idx
            w.readers = []
        self.ops.append(o)
        return o

    def emit(self, stack):
        nc = self.nc
        csem = {}
        for e in ("pe", "act", "dve", "pool"):
            csem[e] = stack.enter_context(nc.semaphore("c_" + e))
        dsems = {}
        for e in ("sp", "pool"):
            dsems[e] = [stack.enter_context(nc.semaphore("d_%s_%d" % (e, i))) for i in range(N_DMA_SEMS[e])]
        ccount = {e: 0 for e in csem}
        dcount = {e: 0 for e in dsems}
        duse = {e: [0] * N_DMA_SEMS[e] for e in dsems}
        for o in self.ops:
            if o.dma:
                k = dcount[o.eng] % N_DMA_SEMS[o.eng]
                dcount[o.eng] += 1
                s = dsems[o.eng][k]
                prev = duse[o.eng][k]
                if prev > 0:
                    o.prewait = (s, 16 * prev)
                duse[o.eng][k] = prev + 1
                o.ticket = (s, 16 * (prev + 1))
            elif o.signal:
                ccount[o.eng] += 1
                o.ticket = (csem[o.eng], ccount[o.eng])
        self.stats = dict(n_ops=len(self.ops), signals=dict(ccount), dmas=dict(dcount))
        block = stack.enter_context(nc.Block())
        per_eng = {e: [o for o in self.ops if o.eng == e] for e in self.ENGS}
        ops = self.ops

        def run(engobj, ename):
            seen = {}
            for o in per_eng[ename]:
                waits = []
                if o.prewait is not None:
                    waits.append(o.prewait)
                for d in sorted(o.deps):
                    waits.append(ops[d].ticket)
                for (s, v) in waits:
                    key = id(s)
                    if seen.get(key, 0) >= v:
                        continue
                    seen[key] = v
                    engobj.wait_ge(s, v)
                ins = o.fn(engobj)
                if o.ticket is not None:
                    ins.then_inc(o.ticket[0], 16 if o.dma else 1)

        @block.sync
        def _(e):
            run(e, "sp")

        @block.scalar
        def _(e):
            run(e, "act")

        @block.vector
        def _(e):
            run(e, "dve")

        @block.gpsimd
        def _(e):
            run(e, "pool")

        @block.tensor
        def _(e):
            run(e, "pe")


VOFF = {}
_o = 0
for _n, _w in [("bmod", 48), ("ssmd", 2), ("qan", 2), ("kvan", 1), ("qn", 1), ("kn", 1), ("won", 8),
               ("convw", 132), ("convb", 44), ("gbias", 256), ("gvn", 256), ("are", 16), ("aim", 16), ("ldt", 16)]:
    VOFF[_n] = _o
    _o += _w
NV = _o


class Part:
    def __init__(self, name, nseq, L, ctx):
        self.name = name
        self.nseq = nseq
        self.L = L
        self.ctx = ctx
        self.N = nseq * L
        self.kvoff = PAST if ctx else 0
        self.NKV = self.N + self.kvoff


def build_program(dec_seq=DEC_SEQ):
    nc = bass.Bass("TRN2", target_bir_lowering=False)
    parts = [Part("p", NSEQ_P, SEQ, False), Part("s", 1, dec_seq, True)]
    NMAX = max(p.N for p in parts)
    NKVMAX = max(p.NKV for p in parts)
    st = ExitStack()
    with st:
        P = Prog(nc)
        st.enter_context(nc.allow_non_contiguous_dma(reason="tiny halo / state / per-feature vector transfers"))

        def din(name, shape, dt=F32):
            return nc.dram_tensor(name, list(shape), dt, kind="ExternalInput").ap()

        def dout(name, shape, dt=F32):
            return nc.dram_tensor(name, list(shape), dt, kind="ExternalOutput").ap()

        def dscr(name, shape, dt):
            return nc.dram_tensor(name, list(shape), dt, kind="Internal").ap()

        def sb(name, shape, dt):
            return st.enter_context(nc.sbuf_tensor("sb_" + name, list(shape), dt))

        xin = {p.name: din("xT_" + p.name, [D, p.N]) for p in parts}
        yout = {p.name: dout("yT_" + p.name, [D, p.N]) for p in parts}
        condT = din("condT", [128, 8, 2])
        cache_ckvT = din("cache_ckvT", [DEPTH, 128, PAST])
        cache_krT = din("cache_krT", [DEPTH, 32, PAST])
        h0in = din("h0", [DEPTH, 128, 16, 2])
        ropeC = din("ropeC", [32, dec_seq])
        ropeS = din("ropeS", [32, dec_seq])
        consts_in = din("consts", [128, 256])
        vecs_in = din("vecs", [DEPTH, 128, NV])
        w_mod_in = din("w_mod_t", [DEPTH, 24, 128, 8 * 256])
        w_in_in = din("w_in_t", [DEPTH, 128, 8 * 1184])
        w_out_in = din("w_out_t", [DEPTH, 128, 8 * 1024])
        w_uq_in = din("w_uq_t", [DEPTH, 128, 2 * 768])
        w_ukvk_in = din("w_ukvk_t", [DEPTH, 128, 8 * 96])
        w_ukvv_in = din("w_ukvv_t", [DEPTH, 128, 512])
        w_glu_in = din("w_glu_t", [DEPTH, 128, 2 * 256])
        w_up_in = din("w_up_t", [DEPTH, NF, 128, 8 * 256])
        w_down_in = din("w_down_t", [DEPTH, 8, 128, NF * 128])
        ssmB_in = din("ssmB_t", [DEPTH, 128, 16 * 2 * 128])
        ssmC_in = din("ssmC_t", [DEPTH, 128, 16 * 2 * 128])
        gws_in = din("gmlp_wsT", [DEPTH, 128, 4 * 128])
        new_ckvT = dout("new_ckvT", [DEPTH, 128, parts[0].N])
        new_krT = dout("new_krT", [DEPTH, 32, parts[0].N])
        new_h = dout("new_h", [DEPTH, NSEQ_P, 128, 16, 2])

        w_mod_b = dscr("w_mod_b", [DEPTH, 24, 128, 8 * 256], BF16)
        w_in_b = dscr("w_in_b", [DEPTH, 128, 8 * 1184], BF16)
        w_out_b = dscr("w_out_b", [DEPTH, 128, 8 * 1024], BF16)
        w_uq_b = dscr("w_uq_b", [DEPTH, 128, 2 * 768], BF16)
        w_up_b = dscr("w_up_b", [DEPTH, NF, 128, 8 * 256], BF16)
        w_down_b = dscr("w_down_b", [DEPTH, 8, 128, NF * 128], BF16)
        ssmB_b = dscr("ssmB_b", [DEPTH, 128, 16 * 2 * 128], BF16)
        ssmC_b = dscr("ssmC_b", [DEPTH, 128, 3 * 16 * 128], BF16)
        scr = {}
        for p in parts:
            scr[p.name] = dict(
                u=dscr("s_u_" + p.name, [256, p.N], BF16),
                q=dscr("s_q_" + p.name, [8, 96, p.N], BF16),
                yc=dscr("s_yc_" + p.name, [256, p.N], BF16),
                ya=dscr("s_ya_" + p.name, [256, p.N], BF16),
                yb=dscr("s_yb_" + p.name, [512, p.N], BF16),
                yf=dscr("s_yf_" + p.name, [256, p.N], F32),
                h2=dscr("s_h2_" + p.name, [D, p.nseq, p.L + 2], BF16),
                xa=dscr("s_xa_" + p.name, [D, p.N], F32),
                xb=dscr("s_xb_" + p.name, [D, p.N], F32),
            )

        vecs = sb("vecs", [128, DEPTH, NV], F32)
        consts = sb("consts", [128, 256], F32)
        cbf = sb("cbf", [128, 256], BF16)
        ones_b = sb("ones_b", [128, 128], BF16)
        ones_f = sb("ones_f", [128, 128], F32)
        epsc = sb("epsc", [128, 1], F32)
        zeros_b = sb("zeros_b", [128, 16], BF16)
        condsb = sb("condsb", [128, 8, 2], F32)
        condb = sb("condb", [128, 8, 2], BF16)
        modT = sb("modT", [128, 48, 2], F32)
        WB = sb("WB", [128, 11264], BF16)
        wsm = sb("wsm", [128, 8 * 96 + 512 + 512 + 512], BF16)
        U1 = sb("U1", [128, 11264], BF16)
        ring_up = [sb("rup%d" % i, [128, 8 * 256], BF16) for i in range(3)]
        xt = [sb("xt%d" % i, [128, 8, 512], F32) for i in range(2)]
        hb = [sb("hb%d" % i, [128, 8, 512], BF16) for i in range(2)]
        rstd = sb("rstd", [128, 512], F32)
        tmpf = [sb("tmpf%d" % i, [128, 512], F32) for i in range(4)]
        tmpb = [sb("tmpb%d" % i, [128, 512], BF16) for i in range(4)]
        f2 = [sb("f2_%d" % i, [128, 2, 512], F32) for i in range(3)]
        b2 = [sb("b2_%d" % i, [128, 2, 512], BF16) for i in range(3)]
        ckvn_res = sb("ckvn_res", [128, NKVMAX], BF16)
        kr_res = sb("kr_res", [32, NKVMAX], BF16)
        ropec_t = sb("ropec_t", [32, 512], F32)
        ropes_t = sb("ropes_t", [32, 512], F32)
        vpad = sb("vpad", [128, 4, 128], BF16)
        smallc = sb("smallc", [128, 8], F32)
        Tcs = sb("Tcs", [128, 16, 2, 128], F32)
        Tc = Tcs[:, :, 0, :]
        Ts = Tcs[:, :, 1, :]
        sc = {n: sb("ssm_" + n, [128, 16], F32) for n in
              ["r", "cT", "sT", "nsT", "fr", "fi", "for", "foi", "fir", "fii", "t0", "t1", "t2", "t3", "t4", "t5",
               "c", "s", "h0r", "h0i"]}
        rr1 = sb("rr1", [128, 16, 2], F32)
        rr2 = sb("rr2", [128, 16, 2], F32)
        carry = sb("carry", [128, 16, 2], F32)
        hfin = sb("hfin", [128, 16, 2], F32)
        sw = [sb("sw%d" % i, [128, 8, 128], F32) for i in range(2)]
        sq4 = [sb("sq4_%d" % i, [128, 4, 128], BF16) for i in range(3)]
        qtile = [sb("qtile%d" % i, [96, 512], BF16) for i in range(2)]
        ptile = [sb("ptile%d" % i, [128, 512], BF16) for i in range(3)]
        osb = sb("osb", [65, 512], F32)
        pbs = [st.enter_context(nc.psum_tensor("pb%d" % i, [128, 512], F32)) for i in range(8)]

        R = {}

        def rs(name):
            if name not in R:
                R[name] = Res(name)
            return R[name]

        pbR = [rs("pb%d" % i) for i in range(8)]

        def DMA(out, in_, rd, wr, eng="sp"):
            return P.op(eng, lambda e: e.dma_start(out=out, in_=in_), reads=rd, writes=wr, dma=True)

        def MM(out, lhsT, rhs, start, stop, rd, wr):
            return P.op("pe", lambda e: e.matmul(out, lhsT=lhsT, rhs=rhs, start=start, stop=stop), reads=rd, writes=wr)

        def ACT(out, in_, func, rd, wr, scale=None, bias=None):
            kw = {}
            if scale is not None:
                kw["scale"] = scale
            if bias is not None:
                kw["bias"] = bias
            return P.op("act", lambda e: e.activation(out=out, in_=in_, func=func, **kw), reads=rd, writes=wr)

        def TT(eng, out, in0, in1, op, rd, wr):
            return P.op(eng, lambda e: e.tensor_tensor(out=out, in0=in0, in1=in1, op=op), reads=rd, writes=wr)

        def STT(out, in0, scalar, in1, op0, op1, rd, wr):
            return P.op("dve", lambda e: e.scalar_tensor_tensor(out=out, in0=in0, scalar=scalar, in1=in1, op0=op0, op1=op1),
                        reads=rd, writes=wr)

        def TS(eng, out, in0, s1, op0, rd, wr, s2=None, op1=None):
            if op1 is None:
                return P.op(eng, lambda e: e.tensor_scalar(out=out, in0=in0, scalar1=s1, scalar2=None, op0=op0), reads=rd, writes=wr)
            return P.op(eng, lambda e: e.tensor_scalar(out=out, in0=in0, scalar1=s1, scalar2=s2, op0=op0, op1=op1), reads=rd, writes=wr)

        def CP(eng, out, in_, rd, wr):
            return P.op(eng, lambda e: e.tensor_copy(out=out, in_=in_), reads=rd, writes=wr)

        def MS(eng, ap, val, wr):
            return P.op(eng, lambda e: e.memset(ap, val), writes=wr)

        def RECIP(out, in_, rd, wr):
            return P.op("dve", lambda e: e.reciprocal(out=out, in_=in_), reads=rd, writes=wr)

        def rstd_from_ss(ps_ap, out_ap, scale, rd, wr):
            np_ = out_ap.shape[0]
            ACT(out_ap, ps_ap, AF.Ln, rd + [rs("epsc")], wr, scale=scale, bias=epsc[0:np_, 0:1])
            ACT(out_ap, out_ap, AF.Exp, wr, wr, scale=-0.5)

        DMA(consts[:], consts_in, [], [rs("consts")])
        CP("dve", cbf[:], consts[:], [rs("consts")], [rs("cbf")])
        pmat = cbf[0:32, 0:32]
        id3296 = cbf[0:32, 32:128]
        MS("dve", ones_b[:], 1.0, [rs("ones_b")])
        MS("dve", ones_f[:], 1.0, [rs("ones_f")])
        MS("dve", epsc[:], EPS, [rs("epsc")])
        MS("dve", zeros_b[:], 0.0, [rs("zeros_b")])
        MS("pool", vpad[:], 0.0, [rs("vpad")])
        DMA(vecs[:], vecs_in.rearrange("l p v -> p l v"), [], [rs("vecs")])
        DMA(condsb[:], condT, [], [rs("condsb")])
        ACT(condb[:], condsb[:], AF.Silu, [rs("condsb")], [rs("condb")])
        cast_ctr = [0]

        def cast(dst, src, name):
            cast_ctr[0] += 1
            DMA(dst, src, [], [rs(name), rs("castslot%d" % (cast_ctr[0] % 4))], eng="pool")

        def cast_layer(l):
            for j in range(0, 24, 4):
                cast(w_mod_b[l, j:j + 4].rearrange("j p (a m) -> j p a m", a=2),
                     w_mod_in[l, j:j + 4].rearrange("j p (a m) -> j p a m", a=2), "w_mod_b%d_%d" % (l, j // 4))
            cast(w_in_b[l].rearrange("p (k m) -> p k m", k=8), w_in_in[l].rearrange("p (k m) -> p k m", k=8), "w_in_b%d" % l)
            cast(w_uq_b[l].rearrange("p (k m) -> p k m", k=2), w_uq_in[l].rearrange("p (k m) -> p k m", k=2), "w_uq_b%d" % l)
            cast(ssmB_b[l].rearrange("p (a m) -> p a m", a=4), ssmB_in[l].rearrange("p (a m) -> p a m", a=4), "ssmB_b%d" % l)
            cast(w_out_b[l].rearrange("p (k m) -> p k m", k=8), w_out_in[l].rearrange("p (k m) -> p k m", k=8), "w_out_b%d" % l)
            for f in range(0, NF, 2):
                cast(w_up_b[l, f:f + 2].rearrange("j p (a m) -> j p a m", a=2),
                     w_up_in[l, f:f + 2].rearrange("j p (a m) -> j p a m", a=2), "w_up_b%d" % l)
            for o in range(0, 8, 2):
                cast(w_down_b[l, o:o + 2].rearrange("j p (a m) -> j p a m", a=4),
                     w_down_in[l, o:o + 2].rearrange("j p (a m) -> j p a m", a=4), "w_down_b%d" % l)

        cast_layer(0)
        cast_layer(1)

        bank_ctr = [0]
        rope_loaded = [None]
        qctr = [0]
        actr = [0]

        def gbank():
            bank_ctr[0] ^= 1
            return 1 + bank_ctr[0]

        for l in range(DEPTH):
            V = lambda n, i=0, w=1, rows=128, l=l: vecs[0:rows, l, VOFF[n] + i:VOFF[n] + i + w]
            Rv = rs("vecs")
            DMA(wsm[:, 0:768], w_ukvk_in[l], [], [rs("wsm")], eng="pool")
            DMA(wsm[:, 768:1280], w_ukvv_in[l], [], [rs("wsm")], eng="pool")
            DMA(wsm[:, 1280:1792], w_glu_in[l], [], [rs("wsm")], eng="pool")
            DMA(wsm[:, 1792:2304], gws_in[l], [], [rs("wsm")], eng="pool")
            wukvk = wsm[:, 0:768].rearrange("p (h m) -> p h m", h=8)
            wukvv = wsm[:, 768:1280]
            wglu = wsm[:, 1280:1792].rearrange("p (k m) -> p k m", k=2)
            gws = wsm[:, 1792:2304].rearrange("p (h q) -> p h q", h=4)

            pm = pbs[0]
            for jb in range(24):
                rb = ring_up[jb % 3]
                rR = rs("rup%d" % (jb % 3))
                DMA(rb[:], w_mod_b[l, jb], [rs("w_mod_b%d_%d" % (l, jb // 4))], [rR])
                rbv = rb[:].rearrange("p (k m) -> p k m", k=8)
                for hf in range(2):
                    j = jb * 2 + hf
                    for k in range(8):
                        MM(pm[:, 2 * j:2 * j + 2], rbv[:, k, hf * 128:(hf + 1) * 128], condb[:, k, :], k == 0, k == 7,
                           [rR, rs("condb")], [pbR[0]])
            for cnd in range(2):
                TT("dve", modT[:, :, cnd], pm[:, 0:96].rearrange("p (j c) -> p j c", c=2)[:, :, cnd], V("bmod", 0, 48),
                   ALU.add, [pbR[0], Rv], [rs("modT")])
            for i in (1, 4):
                TS("dve", modT[:, i * 8:(i + 1) * 8, :], modT[:, i * 8:(i + 1) * 8, :], 1.0, ALU.add, [rs("modT")], [rs("modT")])

            Rs = rs("ssmc")
            are, aim, ldt = V("are", 0, 16), V("aim", 0, 16), V("ldt", 0, 16)
            S = lambda n: sc[n][:]
            ACT(S("t0"), ldt, AF.Exp, [Rv], [Rs])
            TT("dve", S("t1"), are, S("t0"), ALU.mult, [Rv, Rs], [Rs])
            TT("dve", S("t2"), aim, S("t0"), ALU.mult, [Rv, Rs], [Rs])
            ACT(S("r"), S("t1"), AF.Exp, [Rs], [Rs])
            ACT(S("s"), S("t2"), AF.Sin, [Rs], [Rs], scale=1.0 / 8)
            ACT(S("t3"), S("t2"), AF.Sin, [Rs], [Rs], scale=1.0 / 16)
            TT("dve", S("t3"), S("t3"), S("t3"), ALU.mult, [Rs], [Rs])
            TS("dve", S("c"), S("t3"), -2.0, ALU.mult, [Rs], [Rs], 1.0, ALU.add)

            def csq(cn, sn):
                TT("dve", S("t3"), S(cn), S(sn), ALU.mult, [Rs], [Rs])
                TT("dve", S("t4"), S(cn), S(cn), ALU.mult, [Rs], [Rs])
                TT("dve", S("t5"), S(sn), S(sn), ALU.mult, [Rs], [Rs])
                TT("dve", S(cn), S("t4"), S("t5"), ALU.subtract, [Rs], [Rs])
                TS("dve", S(sn), S("t3"), 2.0, ALU.mult, [Rs], [Rs])

            for _ in range(3):
                csq("c", "s")
            RT = rs("Ttab")
            MS("dve", Tc[:, :, 0:1], 1.0, [RT])
            MS("dve", Ts[:, :, 0:1], 0.0, [RT])
            CP("dve", sc["cT"][:], S("c"), [Rs], [Rs])
            CP("dve", sc["sT"][:], S("s"), [Rs], [Rs])
            n = 1
            while n < 128:
                cb = sc["cT"][:, :, None].to_broadcast([128, 16, n])
                sbb = sc["sT"][:, :, None].to_broadcast([128, 16, n])
                flat = xt[1][:].rearrange("p a b -> p (a b)")
                w0 = flat[:, 0:16 * n].rearrange("p (g t) -> p g t", g=16)
                w1 = flat[:, 2048:2048 + 16 * n].rearrange("p (g t) -> p g t", g=16)
                Rw = rs("xt1")
                TT("dve", w0, Tc[:, :, 0:n], cb, ALU.mult, [RT, Rs], [Rw])
                TT("dve", w1, Ts[:, :, 0:n], sbb, ALU.mult, [RT, Rs], [Rw])
                TT("dve", Tc[:, :, n:2 * n], w0, w1, ALU.subtract, [Rw], [RT])
                TT("dve", w0, Tc[:, :, 0:n], sbb, ALU.mult, [RT, Rs], [Rw])
                TT("dve", w1, Ts[:, :, 0:n], cb, ALU.mult, [RT, Rs], [Rw])
                TT("dve", Ts[:, :, n:2 * n], w0, w1, ALU.add, [Rw], [RT])
                csq("cT", "sT")
                n *= 2
            TS("dve", S("nsT"), S("sT"), -1.0, ALU.mult, [Rs], [Rs])
            CP("dve", rr1[:, :, 0], S("r"), [Rs], [Rs])
            MS("dve", rr1[:, :, 1], 1.0, [Rs])
            TS("dve", rr2[:, :, 0], S("r"), -1.0, ALU.mult, [Rs], [Rs])
            MS("dve", rr2[:, :, 1], -1.0, [Rs])
            TT("dve", S("t0"), S("r"), S("c"), ALU.mult, [Rs], [Rs])
            TS("dve", S("t0"), S("t0"), -1.0, ALU.add, [Rs], [Rs])
            TT("dve", S("t1"), S("r"), S("s"), ALU.mult, [Rs], [Rs])
            TT("dve", S("t2"), are, are, ALU.mult, [Rv], [Rs])
            TT("dve", S("t3"), aim, aim, ALU.mult, [Rv], [Rs])
            TT("dve", S("t2"), S("t2"), S("t3"), ALU.add, [Rs], [Rs])
            RECIP(S("t2"), S("t2"), [Rs], [Rs])
            TT("dve", S("t3"), S("t0"), are, ALU.mult, [Rs, Rv], [Rs])
            TT("dve", S("t4"), S("t1"), aim, ALU.mult, [Rs, Rv], [Rs])
            TT("dve", S("t3"), S("t3"), S("t4"), ALU.add, [Rs], [Rs])
            TT("dve", S("fr"), S("t3"), S("t2"), ALU.mult, [Rs], [Rs])
            TT("dve", S("t3"), S("t1"), are, ALU.mult, [Rs, Rv], [Rs])
            TT("dve", S("t4"), S("t0"), aim, ALU.mult, [Rs, Rv], [Rs])
            TT("dve", S("t3"), S("t3"), S("t4"), ALU.subtract, [Rs], [Rs])
            TT("dve", S("fi"), S("t3"), S("t2"), ALU.mult, [Rs], [Rs])
            TT("dve", S("t3"), S("fr"), S("c"), ALU.mult, [Rs], [Rs])
            TT("dve", S("t4"), S("fi"), S("s"), ALU.mult, [Rs], [Rs])
            TT("dve", S("for"), S("t3"), S("t4"), ALU.add, [Rs], [Rs])
            TT("dve", S("t3"), S("fi"), S("c"), ALU.mult, [Rs], [Rs])
            TT("dve", S("t4"), S("fr"), S("s"), ALU.mult, [Rs], [Rs])
            TT("dve", S("foi"), S("t3"), S("t4"), ALU.subtract, [Rs], [Rs])
            TT("dve", S("t0"), S("fr"), S("fr"), ALU.mult, [Rs], [Rs])
            TT("dve", S("t1"), S("fi"), S("fi"), ALU.mult, [Rs], [Rs])
            TT("dve", S("t0"), S("t0"), S("t1"), ALU.add, [Rs], [Rs])
            RECIP(S("t0"), S("t0"), [Rs], [Rs])
            TT("dve", S("t3"), S("c"), S("fr"), ALU.mult, [Rs], [Rs])
            TT("dve", S("t4"), S("s"), S("fi"), ALU.mult, [Rs], [Rs])
            TT("dve", S("t3"), S("t3"), S("t4"), ALU.add, [Rs], [Rs])
            TT("dve", S("fir"), S("t3"), S("t0"), ALU.mult, [Rs], [Rs])
            TT("dve", S("t3"), S("s"), S("fr"), ALU.mult, [Rs], [Rs])
            TT("dve", S("t4"), S("c"), S("fi"), ALU.mult, [Rs], [Rs])
            TT("dve", S("t3"), S("t3"), S("t4"), ALU.subtract, [Rs], [Rs])
            TT("dve", S("fii"), S("t3"), S("t0"), ALU.mult, [Rs], [Rs])
            cf = xt[0][:].rearrange("p a b -> p (a b)")
            Rx0 = rs("xt0")
            DMA(cf, ssmC_in[l], [], [Rx0])
            cfv = cf.rearrange("p (g r m) -> p g r m", g=16, r=2)
            cov = U1[:, 0:6144].rearrange("p (t g m) -> p t g m", t=3, g=16)
            RU = rs("U1")
            for gd in range(16):
                frc, fic = sc["fr"][:, gd:gd + 1], sc["fi"][:, gd:gd + 1]
                t_a, t_b = tmpf[0][:, 0:128], tmpf[1][:, 0:128]
                Rt = [rs("tmpf0"), rs("tmpf1")]
                TS("dve", t_a, cfv[:, gd, 1, :], fic, ALU.mult, [Rx0, Rs], [Rt[0]])
                STT(t_b, cfv[:, gd, 0, :], frc, t_a, ALU.mult, ALU.subtract, [Rx0, Rs, Rt[0]], [Rt[1]])
                CP("dve", cov[:, 0, gd, :], t_b, [Rt[1]], [RU])
                TS("dve", cov[:, 1, gd, :], t_b, -1.0, ALU.mult, [Rt[1]], [RU])
                TS("dve", t_a, cfv[:, gd, 1, :], frc, ALU.mult, [Rx0, Rs], [Rt[0]])
                STT(t_b, cfv[:, gd, 0, :], fic, t_a, ALU.mult, ALU.add, [Rx0, Rs, Rt[0]], [Rt[1]])
                TS("dve", cov[:, 2, gd, :], t_b, -1.0, ALU.mult, [Rt[1]], [RU])
            DMA(ssmC_b[l], U1[:, 0:6144], [RU], [rs("ssmC_b%d" % l)])

            for p in parts:
                pn = p.name
                N, L = p.N, p.L
                S_ = scr[pn]
                ntile = (N + 511) // 512
                x_src = xin[pn] if l == 0 else S_["xb"]
                x_dst = yout[pn] if l == DEPTH - 1 else S_["xb"]
                md = 0 if not p.ctx else 1
                MOD = lambda i, c, md=md: modT[:, i * 8 + c, md:md + 1]
                Rm = rs("modT")
                tR = lambda nm, t: rs("%s_%s_%d" % (nm, pn, t))

                if p.ctx:
                    DMA(ckvn_res[:, 0:PAST], cache_ckvT[l], [], [rs("ckvres_c")], eng="pool")
                    DMA(kr_res[:, 0:PAST], cache_krT[l], [], [rs("krres_c")], eng="pool")

                RW = rs("WB")
                DMA(WB[:, 0:8 * 1184], w_in_b[l], [rs("w_in_b%d" % l)], [RW])
                DMA(WB[:, 8 * 1184:8 * 1184 + 1536], w_uq_b[l], [rs("w_uq_b%d" % l)], [RW])
                winT = WB[:, 0:8 * 1184].rearrange("p (k m) -> p k m", k=8)
                wuq = WB[:, 8 * 1184:8 * 1184 + 1536].rearrange("p (k m) -> p k m", k=2)

                def qk_gen(produce_raw, gaincol, out_ap, outR, n, rope_pos, par, finish, extraW=(), banks=((3, 4), (6, 7))):
                    pq_b, aux_b = banks[par]
                    psrc, psR = pbs[pq_b][0:96, 0:n], pbR[pq_b]
                    produce_raw(psrc, psR)
                    yield
                    sqk = tmpb[par][0:96, 0:n]
                    Rsqk = rs("tmpb%d" % par)
                    ACT(sqk, psrc, AF.Square, [psR], [Rsqk])
                    yield
                    MM(pbs[aux_b][0:96, 0:n], ones_b[0:96, 0:96], sqk, True, True, [Rsqk, rs("ones_b")], [pbR[aux_b]])
                    yield
                    ri_ = 2 if par == 0 else 0
                    rsq = tmpf[ri_][0:96, 0:n]
                    Rrsq = rs("tmpf%d" % ri_)
                    rstd_from_ss(pbs[aux_b][0:96, 0:n], rsq, 1.0 / 96, [pbR[aux_b]], [Rrsq])
                    yield
                    STT(out_ap, psrc, gaincol, rsq, ALU.mult, ALU.mult, [psR, Rv, Rrsq], [outR] + list(extraW))
                    yield
                    if rope_pos is not None:
                        if rope_loaded[0] != (rope_pos, n):
                            rope_loaded[0] = (rope_pos, n)
                            DMA(ropec_t[:, 0:n], ropeC[:, rope_pos:rope_pos + n], [], [rs("ropec")])
                            DMA(ropes_t[:, 0:n], ropeS[:, rope_pos:rope_pos + n], [], [rs("ropes")])
                        MM(pbs[aux_b][0:32, 0:n], pmat, out_ap[0:32, :], True, True, [outR, rs("cbf")], [pbR[aux_b]])
                        yield
                        ti_ = 3 if par == 0 else 1
                        t1 = tmpf[ti_][0:32, 0:n]
                        t2 = tmpf[ri_][0:32, 0:n]
                        TT("dve", t1, out_ap[0:32, :], ropec_t[:, 0:n], ALU.mult, [outR, rs("ropec")], [rs("tmpf%d" % ti_)])
                        TT("dve", t2, pbs[aux_b][0:32, 0:n], ropes_t[:, 0:n], ALU.mult, [pbR[aux_b], rs("ropes")], [Rrsq])
                        TT("dve", out_ap[0:32, :], t1, t2, ALU.add, [rs("tmpf%d" % ti_), Rrsq], [outR])
                        yield
                    finish()

                def run_rr_gen(gen_makers):
                    free = [0, 1]
                    active = []
                    pendm = list(gen_makers)
                    while pendm or active:
                        while pendm and free:
                            par = free.pop(0)
                            active.append((par, pendm.pop(0)(par)))
                        for (par, g) in list(active):
                            try:
                                next(g)
                            except StopIteration:
                                active.remove((par, g))
                                free.append(par)
                        yield

                def drive(gA, gB, nB):
                    doneA = doneB = False
                    while not (doneA and doneB):
                        if not doneA:
                            try:
                                next(gA)
                            except StopIteration:
                                doneA = True
                        for _ in range(nB):
                            if doneB:
                                break
                            try:
                                next(gB)
                            except StopIteration:
                                doneB = True

                def run_rr(gen_makers):
                    free = [0, 1]
                    active = []
                    pendm = list(gen_makers)
                    while pendm or active:
                        while pendm and free:
                            par = free.pop(0)
                            active.append((par, pendm.pop(0)(par)))
                        for (par, g) in list(active):
                            try:
                                next(g)
                            except StopIteration:
                                active.remove((par, g))
                                free.append(par)

                def p1_load(t_):
                    c0_ = t_ * 512
                    n_ = min(512, N - c0_)
                    DMA(xt[t_ % 2][:, :, 0:n_], x_src.rearrange("(c p) n -> p c n", p=128)[:, :, c0_:c0_ + n_],
                        [tR("x", t_)] if l > 0 else [], [rs("xt%d" % (t_ % 2))])

                p1_load(0)
                for t in range(ntile):
                    c0 = t * 512
                    n = min(512, N - c0)
                    X = xt[t % 2]
                    RX = rs("xt%d" % (t % 2))
                    if t + 1 < ntile:
                        p1_load(t + 1)
                    SQ = hb[1]
                    RSQ = rs("hb1")
                    ACT(SQ[:, :, 0:n], X[:, :, 0:n], AF.Square, [RX], [RSQ])
                    for c in range(8):
                        MM(pbs[0][:, 0:n], ones_b[:], SQ[:, c, 0:n], c == 0, c == 7, [RSQ, rs("ones_b")], [pbR[0]])
                    rstd_from_ss(pbs[0][:, 0:n], rstd[:, 0:n], 1.0 / D, [pbR[0]], [rs("rstd")])
                    H = hb[0]
                    RH = rs("hb0")
                    for c in range(8):
                        tf = tmpf[c % 2]
                        Rtf = rs("tmpf%d" % (c % 2))
                        STT(tf[:, 0:n], X[:, c, 0:n], MOD(1, c), rstd[:, 0:n], ALU.mult, ALU.mult, [RX, Rm, rs("rstd")], [Rtf])
                        ACT(H[:, c, 0:n], tf[:, 0:n], AF.Identity, [Rtf, Rm], [RH], bias=MOD(0, c))

                    def zchunk(col0, m):
                        b = gbank()
                        for k in range(8):
                            MM(pbs[b][0:m, 0:n], winT[:, k, col0:col0 + m], H[:, k, 0:n], k == 0, k == 7, [RW, RH], [pbR[b]])
                        return b

                    UB = b2[0]
                    for j in range(2):
                        b = zchunk(j * 128, 128)
                        ACT(UB[:, j, 0:n], pbs[b][:, 0:n], AF.Copy, [pbR[b]], [rs("b2_0")])
                    DMA(S_["u"].rearrange("(c p) n -> p c n", p=128)[:, :, c0:c0 + n], UB[:, :, 0:n], [rs("b2_0")], [tR("u", t)])
                    CQ = f2[0]
                    for j in range(2):
                        b = zchunk(256 + j * 128, 128)
                        ACT(CQ[:, j, 0:n], pbs[b][:, 0:n], AF.Copy, [pbR[b]], [rs("f2_0")])
                        ACT(b2[2][:, j, 0:n], pbs[b][:, 0:n], AF.Square, [pbR[b]], [rs("b2_2")])
                    for j in range(2):
                        MM(pbs[0][:, 0:n], ones_b[:], b2[2][:, j, 0:n], j == 0, j == 1, [rs("b2_2"), rs("ones_b")], [pbR[0]])
                    rstd_from_ss(pbs[0][:, 0:n], rstd[:, 0:n], 1.0 / 256, [pbR[0]], [rs("rstd")])
                    CQN = b2[1]
                    for j in range(2):
                        STT(CQN[:, j, 0:n], CQ[:, j, 0:n], V("qan", j), rstd[:, 0:n], ALU.mult, ALU.mult,
                            [rs("f2_0"), Rv, rs("rstd")], [rs("b2_1")])
                    b = zchunk(512, 128)
                    ACT(tmpf[0][:, 0:n], pbs[b][:, 0:n], AF.Copy, [pbR[b]], [rs("tmpf0")])
                    ACT(tmpb[1][:, 0:n], pbs[b][:, 0:n], AF.Square, [pbR[b]], [rs("tmpb1")])
                    MM(pbs[0][:, 0:n], ones_b[:], tmpb[1][:, 0:n], True, True, [rs("tmpb1"), rs("ones_b")], [pbR[0]])
                    rstd_from_ss(pbs[0][:, 0:n], rstd[:, 0:n], 1.0 / 128, [pbR[0]], [rs("rstd")])
                    STT(tmpf[1][:, 0:n], tmpf[0][:, 0:n], V("kvan", 0), rstd[:, 0:n], ALU.mult, ALU.mult,
                        [rs("tmpf0"), Rv, rs("rstd")], [rs("tmpf1")])
                    kvR = tR("ckvres", t)
                    ACT(ckvn_res[:, p.kvoff + c0:p.kvoff + c0 + n], tmpf[1][:, 0:n], AF.Copy, [rs("tmpf1")], [kvR])
                    if not p.ctx:
                        DMA(new_ckvT[l, :, c0:c0 + n], tmpf[1][:, 0:n], [rs("tmpf1")], [rs("out_ckv")])
                    b = zchunk(1152, 32)
                    ACT(tmpf[3][0:32, 0:n], pbs[b][0:32, 0:n], AF.Copy, [pbR[b]], [rs("tmpf3")])
                    krR = tR("krres", t)
                    ACT(kr_res[:, p.kvoff + c0:p.kvoff + c0 + n], tmpf[3][0:32, 0:n], AF.Copy, [rs("tmpf3")], [krR])
                    if not p.ctx:
                        DMA(new_krT[l, :, c0:c0 + n], tmpf[3][0:32, 0:n], [rs("tmpf3")], [rs("out_kr")])
                    UF = f2[1]
                    for j in range(2):
                        b = zchunk(640 + j * 128, 128)
                        ACT(UF[:, j, 0:n], pbs[b][:, 0:n], AF.Copy, [pbR[b]], [rs("f2_1")])
                    YC = f2[2]
                    for s4 in range(n // 128):
                        tsl = slice(s4 * 128, (s4 + 1) * 128)
                        pv = pbs[6]
                        for k in range(8):
                            MM(pv[:, 0:256], H[:, k, tsl], winT[:, k, 896:1152], k == 0, k == 7, [RW, RH], [pbR[6]])
                        ACT(tmpb[2][:, 0:256], pv[:, 0:256], AF.Square, [pbR[6]], [rs("tmpb2")])
                        P.op("dve", lambda e: e.tensor_reduce(out=smallc[:, 0:1], in_=tmpb[2][:, 0:256], axis=AX.X, op=ALU.add),
                             reads=[rs("tmpb2")], writes=[rs("smallc")])
                        rstd_from_ss(smallc[:, 0:1], smallc[:, 1:2], 1.0 / 256, [rs("smallc")], [rs("smallc")])
                        pv3 = pv[:, 0:256].rearrange("p (h c) -> p h c", h=4)
                        gv3 = V("gvn", 0, 256).rearrange("p (h c) -> p h c", h=4)
                        for par in range(2):
                            hs = slice(par, 4, 2)
                            STT(vpad[:, hs, par * 64:(par + 1) * 64], pv3[:, hs, :], smallc[:, 1:2], gv3[:, hs, :],
                                ALU.mult, ALU.mult, [pbR[6], rs("smallc"), Rv], [rs("vpad")])
                        for j in range(2):
                            pmx = pbs[7]
                            for hh in (2 * j, 2 * j + 1):
                                MM(pmx[:, 0:128], vpad[:, hh, :], gws[:, hh, :], hh == 2 * j, hh == 2 * j + 1,
                                   [rs("vpad"), rs("wsm")], [pbR[7]])
                            TT("dve", tmpf[0][:, 0:128], pmx[:, 0:128], V("gbias", j * 128, 128), ALU.add, [pbR[7], Rv], [rs("tmpf0")])
                            TT("dve", YC[:, j, tsl], tmpf[0][:, 0:128], UF[:, j, tsl], ALU.mult, [rs("tmpf0"), rs("f2_1")], [rs("f2_2")])
                    for j in range(2):
                        ACT(b2[2][:, j, 0:n], YC[:, j, 0:n], AF.Square, [rs("f2_2")], [rs("b2_2")])
                    for j in range(2):
                        MM(pbs[0][:, 0:n], ones_b[:], b2[2][:, j, 0:n], j == 0, j == 1, [rs("b2_2"), rs("ones_b")], [pbR[0]])
                    rstd_from_ss(pbs[0][:, 0:n], rstd[:, 0:n], 1.0 / 256, [pbR[0]], [rs("rstd")])
                    for j in range(2):
                        STT(b2[2][:, j, 0:n], YC[:, j, 0:n], V("won", 6 + j), rstd[:, 0:n], ALU.mult, ALU.mult,
                            [rs("f2_2"), Rv, rs("rstd")], [rs("b2_2")])
                    DMA(S_["yc"].rearrange("(c p) n -> p c n", p=128)[:, :, c0:c0 + n], b2[2][:, :, 0:n], [rs("b2_2")], [tR("yc", t)])
                    def q_maker(hh, n=n, c0=c0, t=t, CQN=CQN):
                        def mk(par):
                            qo = qtile[par]
                            qR = rs("qtile%d" % par)

                            def raw(psrc, psR):
                                for k in range(2):
                                    MM(psrc, wuq[:, k, hh * 96:(hh + 1) * 96], CQN[:, k, 0:n], k == 0, k == 1, [RW, rs("b2_1")], [psR])

                            def fin():
                                DMA(S_["q"][hh, :, c0:c0 + n], qo[:, 0:n], [qR], [tR("q%d" % hh, t)])
                            return qk_gen(raw, V("qn", 0, 1, 96), qo[:, 0:n], qR, n, c0 if p.ctx else None, par, fin)
                        return mk
                    run_rr([q_maker(hh) for hh in range(8)])

                def ssm_phase():
                    DMA(WB[:, 0:4096], ssmB_b[l], [rs("ssmB_b%d" % l)], [RW])
                    DMA(WB[:, 4096:4096 + 6144], ssmC_b[l], [rs("ssmC_b%d" % l)], [RW])
                    Bp = WB[:, 0:4096].rearrange("p (g r m) -> p g r m", g=16, r=2)
                    Cm = WB[:, 4096:4096 + 6144].rearrange("p (t g m) -> p t g m", t=3, g=16)
                    nch_seq = L // 128
                    Rc = rs("carry")
                    rt1 = xt[0][:].rearrange("p a b -> p (a b)")
                    rt2 = xt[1][:].rearrange("p a b -> p (a b)")
                    Rrt = [rs("xt0"), rs("xt1")]
                    for (rt_, Rr_, sgn, tab) in ((rt1, Rrt[0], 1.0, rr1), (rt2, Rrt[1], -1.0, rr2)):
                        v4 = rt_.rearrange("p (g t two) -> p g t two", g=16, two=2)
                        CP("pool", v4[:, :, :, 0], tab[:, :, 0:1].to_broadcast([128, 16, 128]), [Rs], [Rr_])
                        MS("pool", v4[:, :, :, 1], sgn, [Rr_])
                    if p.ctx:
                        DMA(carry[:], h0in[l], [], [Rc])
                        CP("dve", sc["h0r"][:], carry[:, :, 0], [Rc], [Rs])
                        CP("dve", sc["h0i"][:], carry[:, :, 1], [Rc], [Rs])
                    for d in range(2):
                        cur = dict(tile=None)
                        pending = []
                        cctr = [0]

                        pending2 = []

                        def flush(full=True):
                            nb2 = len(pending2)
                            while pending:
                                pending.pop(0)()
                            for _ in range(nb2 if not full else len(pending2)):
                                pending2.pop(0)()

                        def make_B(gp, gd, half, wk, Rwk, ypsum, yb_, d, tc_, ts_, uidx, tail):
                            def Bfn():
                                wkf = wk[:].rearrange("p a b -> p (a b)")
                                grl, gil = wkf[:, 767:768], wkf[:, 1023:1024]
                                g2 = wkf[:, 512:1024].rearrange("p (r t two) -> p r t two", r=2, two=2)[:, :, :, 1]
                                cTc, sTc, nsTc = sc["cT"][:, gd:gd + 1], sc["sT"][:, gd:gd + 1], sc["nsT"][:, gd:gd + 1]
                                Rcg = rs("carry%d" % gd)
                                c0_ = 2 + 2 * (uidx % 2)
                                Rsm = rs("sca%d" % (uidx % 2))
                                ACT(smallc[:, c0_:c0_ + 1], gil, AF.Identity, [Rwk, Rs], [Rsm], scale=nsTc)
                                ACT(smallc[:, c0_ + 1:c0_ + 2], gil, AF.Identity, [Rwk, Rs], [Rsm], scale=cTc)
                                ACT(carry[:, gd, 0:1], grl, AF.Identity, [Rwk, Rs, Rsm], [Rcg], scale=cTc, bias=smallc[:, c0_:c0_ + 1])
                                ACT(carry[:, gd, 1:2], grl, AF.Identity, [Rwk, Rs, Rsm], [Rcg], scale=sTc, bias=smallc[:, c0_ + 1:c0_ + 2])
                                q4 = sq4[uidx % 3]
                                Rq4 = rs("sq4_%d" % (uidx % 3))
                                tcs = Tcs[:, gd, :, :]
                                if d == 0:
                                    oa, ob_ = q4[:, 0:2, :], q4[:, 2:4, :]
                                else:
                                    oa, ob_ = q4[:, 0:2, ::-1], q4[:, 2:4, ::-1]
                                TT("pool", oa, g2, tcs, ALU.mult, [Rwk, RT], [Rq4])
                                TT("pool", ob_, g2, tcs[:, ::-1, :], ALU.mult, [Rwk, RT], [Rq4])

                                def B2fn():
                                    for i, ti in enumerate((0, 1, 2, 2)):
                                        MM(ypsum[:, half, :], Cm[:, ti, gd, :], q4[:, i, :], (gp % 4 == 0 and i == 0), (gp % 4 == 3 and i == 3),
                                           [RW, Rq4], [yb_])
                                    if tail is not None:
                                        tail()
                                pending2.append(B2fn)
                            return Bfn

                        def make_tail(YT, RYT, UBt, RUB, ypsum, yb_, o0, t, tn, done_tile, d):
                            def tail():
                                if d == 0:
                                    CP("dve", YT[:, :, o0:o0 + 128], ypsum, [yb_], [RYT])
                                else:
                                    TT("dve", YT[:, :, o0:o0 + 128], YT[:, :, o0:o0 + 128], ypsum, ALU.add, [yb_, RYT], [RYT])
                                if not done_tile:
                                    return
                                if d == 0:
                                    DMA(S_["yf"].rearrange("(c p) n -> p c n", p=128)[:, :, t * 512:t * 512 + tn], YT[:, :, 0:tn],
                                        [RYT], [tR("yf", t)])
                                    return
                                for j in range(2):
                                    STT(YT[:, j, 0:tn], UBt[:, j, 0:tn], V("ssmd", j), YT[:, j, 0:tn], ALU.mult, ALU.add,
                                        [RUB, Rv, RYT], [RYT])
                                GF = f2[2]
                                GB = b2[2]
                                ACT(GF[:, :, 0:tn], YT[:, :, 0:tn], AF.Gelu, [RYT], [rs("f2_2")])
                                CP("pool", GB[:, :, 0:tn], GF[:, :, 0:tn], [rs("f2_2")], [rs("b2_2")])
                                for j in range(2):
                                    b = 2
                                    for k in range(2):
                                        MM(pbs[b][:, 0:tn], wglu[:, k, j * 128:(j + 1) * 128], GB[:, k, 0:tn], k == 0, k == 1,
                                           [rs("wsm"), rs("b2_2")], [pbR[b]])
                                    ACT(tmpf[0][:, 0:tn], pbs[b][:, 0:tn], AF.Sigmoid, [pbR[b]], [rs("tmpf0")])
                                    TT("dve", GF[:, j, 0:tn], GF[:, j, 0:tn], tmpf[0][:, 0:tn], ALU.mult, [rs("f2_2"), rs("tmpf0")], [rs("f2_2")])
                                for j in range(2):
                                    ACT(GB[:, j, 0:tn], GF[:, j, 0:tn], AF.Square, [rs("f2_2")], [rs("b2_2")])
                                for j in range(2):
                                    MM(pbs[2][:, 0:tn], ones_b[:], GB[:, j, 0:tn], j == 0, j == 1, [rs("b2_2"), rs("ones_b")], [pbR[2]])
                                rstd_from_ss(pbs[2][:, 0:tn], rstd[:, 0:tn], 1.0 / 256, [pbR[2]], [rs("rstd")])
                                for j in range(2):
                                    STT(GB[:, j, 0:tn], GF[:, j, 0:tn], V("won", j), rstd[:, 0:tn], ALU.mult, ALU.mult,
                                        [rs("f2_2"), Rv, rs("rstd")], [rs("b2_2")])
                                DMA(S_["ya"].rearrange("(c p) n -> p c n", p=128)[:, :, t * 512:t * 512 + tn], GB[:, :, 0:tn],
                                    [rs("b2_2")], [tR("ya", t)])
                            return tail

                        uctr = [0]
                        for s in range(p.nseq):
                            flush()
                            if p.ctx:
                                TT("dve", S("t3"), S("fir"), S("h0r"), ALU.mult, [Rs], [Rs])
                                TT("dve", S("t4"), S("fii"), S("h0i"), ALU.mult, [Rs], [Rs])
                                TT("dve", carry[:, d:16:2, 0], S("t3")[:, d:16:2], S("t4")[:, d:16:2], ALU.subtract, [Rs], [Rc])
                                TT("dve", S("t3"), S("fir"), S("h0i"), ALU.mult, [Rs], [Rs])
                                TT("dve", S("t4"), S("fii"), S("h0r"), ALU.mult, [Rs], [Rs])
                                TT("dve", carry[:, d:16:2, 1], S("t3")[:, d:16:2], S("t4")[:, d:16:2], ALU.add, [Rs], [Rc])
                            else:
                                MS("dve", carry[:, d:16:2, :], 0.0, [Rc])
                            chunks = list(range(nch_seq))
                            if d == 1:
                                chunks = chunks[::-1]
                            for ci, ch in enumerate(chunks):
                                g0 = s * L + ch * 128
                                t = g0 // 512
                                o0 = g0 - t * 512
                                tn = min(512, N - t * 512)
                                if cur["tile"] != t:
                                    cur["tile"] = t
                                    cur["UBt"] = b2[(t + d) % 2]
                                    cur["RUB"] = rs("b2_%d" % ((t + d) % 2))
                                    DMA(cur["UBt"][:, :, 0:tn], S_["u"].rearrange("(c p) n -> p c n", p=128)[:, :, t * 512:t * 512 + tn],
                                        [tR("u", t)], [cur["RUB"]])
                                    cur["YT"] = f2[(t + d) % 2]
                                    cur["RYT"] = rs("f2_%d" % ((t + d) % 2))
                                    if d == 1:
                                        DMA(cur["YT"][:, :, 0:tn], S_["yf"].rearrange("(c p) n -> p c n", p=128)[:, :, t * 512:t * 512 + tn],
                                            [tR("yf", t)], [cur["RYT"]])
                                UBt, RUB, YT, RYT = cur["UBt"], cur["RUB"], cur["YT"], cur["RYT"]
                                cctr[0] += 1
                                yb_ = pbR[2]
                                ypsum = pbs[2][:, 0:256].rearrange("p (h n) -> p h n", h=2)
                                last_in_tile = (ci == len(chunks) - 1) or ((s * L + chunks[ci + 1] * 128) // 512 != t)
                                done_tile = last_in_tile and not any(
                                    ((s2 * L) // 512 == t or ((s2 + 1) * L - 1) // 512 == t) for s2 in range(s + 1, p.nseq))
                                for gp in range(8):
                                    gd = gp * 2 + d
                                    half = gp // 4
                                    uidx = uctr[0]
                                    uctr[0] += 1
                                    wk = sw[uidx % 2]
                                    Rwk = rs("sw%d" % (uidx % 2))
                                    hb_ = uidx % 2
                                    Rpb = pbR[hb_]
                                    pb2 = pbs[hb_][:, 0:256].rearrange("p (r n) -> p r n", r=2)
                                    for r_ in range(2):
                                        MM(pb2[:, r_, :], Bp[:, gd, r_, :], UBt[:, half, o0:o0 + 128], True, True, [RW, RUB], [Rpb])
                                    if d == 0:
                                        pbr, pbi = pb2[:, 0, :], pb2[:, 1, :]
                                    else:
                                        pbr, pbi = pb2[:, 0, ::-1], pb2[:, 1, ::-1]
                                    tcs = Tcs[:, gd, :, :]
                                    tc_, ts_ = None, None
                                    if d == 0:
                                        pa, pbsw = pb2, pb2[:, ::-1, :]
                                    else:
                                        pa, pbsw = pb2[:, :, ::-1], pb2[:, ::-1, ::-1]
                                    wkf = wk[:].rearrange("p a b -> p (a b)")
                                    iv = lambda off: wkf[:, off:off + 256].rearrange("p (t two) -> p two t", two=2)
                                    TT("dve", iv(0), pa, tcs, ALU.mult, [Rpb, RT], [Rwk])
                                    TT("dve", iv(256), pa, tcs[:, ::-1, :], ALU.mult, [Rpb, RT], [Rwk])
                                    Rcg = rs("carry%d" % gd)
                                    for (off, dst, ri, rt_, Rr_) in ((0, 512, 0, rt1, Rrt[0]), (256, 768, 1, rt2, Rrt[1])):
                                        P.op("dve", lambda e, off=off, dst=dst, ri=ri, wkf=wkf, rt_=rt_, gd=gd: e.tensor_tensor_scan(
                                            out=wkf[:, dst:dst + 256], data0=rt_[:, gd * 256:(gd + 1) * 256], data1=wkf[:, off:off + 256],
                                            initial=carry[:, gd, ri:ri + 1], op0=ALU.mult, op1=ALU.add),
                                            reads=[Rwk, Rr_, Rc, Rcg], writes=[Rwk])
                                    tail = None
                                    if gp == 7:
                                        tail = make_tail(YT, RYT, UBt, RUB, ypsum, yb_, o0, t, tn, done_tile, d)
                                    Bc = make_B(gp, gd, half, wk, Rwk, ypsum, yb_, d, tc_, ts_, uidx, tail)
                                    flush(full=False)
                                    pending.append(Bc)
                                    yield
                            flush()
                            if not p.ctx:
                                Rcs = [Rc] + [rs("carry%d" % (gp * 2 + d)) for gp in range(8)]
                                cr_, ci_ = carry[:, d:16:2, 0], carry[:, d:16:2, 1]
                                fo_r, fo_i = sc["for"][:, d:16:2], sc["foi"][:, d:16:2]
                                a_, b_ = sc["t3"][:, 0:8], sc["t4"][:, 0:8]
                                Rh = rs("hfin")
                                TT("dve", a_, fo_r, cr_, ALU.mult, [Rs] + Rcs, [Rs])
                                TT("dve", b_, fo_i, ci_, ALU.mult, [Rs] + Rcs, [Rs])
                                TT("dve", hfin[:, d:16:2, 0], a_, b_, ALU.subtract, [Rs], [Rh])
                                TT("dve", a_, fo_r, ci_, ALU.mult, [Rs] + Rcs, [Rs])
                                TT("dve", b_, fo_i, cr_, ALU.mult, [Rs] + Rcs, [Rs])
                                TT("dve", hfin[:, d:16:2, 1], a_, b_, ALU.add, [Rs], [Rh])
                                DMA(new_h[l, s][:, d:16:2, :], hfin[:, d:16:2, :], [Rh], [rs("out_h")])


                def attn_phase():
                    RU = rs("U1")
                    Kh = U1[0:96, 0:p.NKV]
                    nkt = p.NKV // 128
                    Vh = U1[:, 4352:4352 + nkt * 65].rearrange("p (t c) -> p t c", c=65)
                    fin_pending = []
                    fin2_pending = []
                    for hh in range(8):
                        blocks = []
                        if p.ctx:
                            blocks.append((0, PAST, None, [rs("ckvres_c"), rs("krres_c")]))
                        for t in range(ntile):
                            n = min(512, N - t * 512)
                            blocks.append((p.kvoff + t * 512, n, (t * 512) if p.ctx else None, [tR("ckvres", t), tR("krres", t)]))
                        def k_maker(k0, kn, rpos, rr, hh=hh):
                            def mk(par):
                                RK = rs("Kp%d" % par)

                                def raw(psrc, psR):
                                    MM(psrc, wukvk[:, hh, :], ckvn_res[:, k0:k0 + kn], True, False, [rs("wsm")] + rr, [psR])
                                    MM(psrc, id3296, kr_res[:, k0:k0 + kn], False, True, [rs("cbf")] + rr, [psR])

                                def fin():
                                    for k4 in range(0, kn // 128, 4):
                                        nk4 = min(4, kn // 128 - k4)
                                        pv = pbs[7][:, 0:256].rearrange("p (t c) -> p t c", c=64)
                                        for i in range(nk4):
                                            kk = k0 + (k4 + i) * 128
                                            MM(pv[:, i, :], ckvn_res[:, kk:kk + 128], wukvv[:, hh * 64:(hh + 1) * 64], True, True,
                                               [rs("wsm")] + rr, [pbR[7]])
                                        kt0 = k0 // 128 + k4
                                        CP("dve", Vh[:, kt0:kt0 + nk4, 0:64], pv[:, 0:nk4, :], [pbR[7]], [RU])
                                return qk_gen(raw, V("kn", 0, 1, 96), Kh[:, k0:k0 + kn], RK, kn, rpos, par, fin, extraW=[RU], banks=((3, 4), (5, 6)))
                            return mk
                        yield from run_rr_gen([k_maker(*blk) for blk in blocks])
                        MS("pool", Vh[:, :, 64:65], 1.0, [RU])
                        for s in range(p.nseq):
                            if p.ctx:
                                ktl = list(range(nkt))
                            else:
                                ktl = list(range(s * L // 128, (s + 1) * L // 128))
                            for q0 in range(s * L, (s + 1) * L, 512):
                                nq = min(512, (s + 1) * L - q0)
                                t = q0 // 512
                                qctr[0] += 1
                                qi = qctr[0] % 2
                                QT = qtile[qi]
                                RQ = rs("qtile%d" % qi)
                                DMA(QT[:, 0:nq], S_["q"][hh, :, q0:q0 + nq], [tR("q%d" % hh, t)], [RQ])
                                pob = 6 + (qctr[0] % 2)
                                po = pbs[pob]
                                pendq = []
                                nkl = len(ktl)
                                sbank = {}

                                def score(i):
                                    b = 3 + (actr[0] % 3)
                                    actr[0] += 1
                                    sbank[i] = b
                                    MM(pbs[b][:, 0:nq], Kh[:, ktl[i] * 128:(ktl[i] + 1) * 128], QT[:, 0:nq], True, True, [RU, rs("Kp0"), rs("Kp1"), RQ], [pbR[b]])

                                score(0)
                                if nkl > 1:
                                    score(1)
                                for i, kt in enumerate(ktl):
                                    if i + 2 < nkl:
                                        score(i + 2)
                                    b = sbank[i]
                                    pt = ptile[i % 3]
                                    Rpt = rs("ptile%d" % (i % 3))
                                    ACT(pt[:, 0:nq], pbs[b][:, 0:nq], AF.Exp, [pbR[b]], [Rpt], scale=ATTN_SCALE)
                                    if i >= 1:
                                        pi, pkt, ppt, pRpt = pendq.pop(0)
                                        MM(po[0:65, 0:nq], Vh[:, pkt, :], ppt[:, 0:nq], pi == 0, False, [RU, pRpt], [pbR[pob]])
                                    if i == min(4, nkl - 1) and fin_pending:
                                        fin2_pending.append(fin_pending.pop(0)())
                                    if i == min(24, nkl - 1) and fin2_pending:
                                        fin2_pending.pop(0)()
                                    pendq.append((i, kt, pt, Rpt))
                                    yield
                                pi, pkt, ppt, pRpt = pendq.pop(0)
                                MM(po[0:65, 0:nq], Vh[:, pkt, :], ppt[:, 0:nq], pi == 0, True, [RU, pRpt], [pbR[pob]])

                                def make_fin(po, pob, nq, hh, q0, t):
                                    def fin():
                                        CP("dve", osb[:, 0:nq], po[0:65, 0:nq], [pbR[pob]], [rs("osb")])
                                        RECIP(osb[64:65, 0:nq], osb[64:65, 0:nq], [rs("osb")], [rs("osb")])

                                        def fin2():
                                            bc = pob
                                            MM(pbs[bc][0:64, 0:nq], ones_f[64:65, 0:64], osb[64:65, 0:nq], True, True, [rs("osb"), rs("ones_f")], [pbR[bc]])
                                            ob = tmpb[3]
                                            TT("dve", ob[0:64, 0:nq], osb[0:64, 0:nq], pbs[bc][0:64, 0:nq], ALU.mult, [rs("osb"), pbR[bc]], [rs("tmpb3")])
                                            DMA(S_["yb"][hh * 64:(hh + 1) * 64, q0:q0 + nq], ob[0:64, 0:nq], [rs("tmpb3")], [tR("yb%d" % hh, t)])
                                        return fin2
                                    return fin
                                fin_pending.append(make_fin(po, pob, nq, hh, q0, t))
                        while fin_pending:
                            fin2_pending.append(fin_pending.pop(0)())
                        while fin2_pending:
                            fin2_pending.pop(0)()


                n_ssm = max(1, N // 8)
                n_att = 8 * (sum(len(range(s_ * L, (s_ + 1) * L, 512)) * ((p.NKV // 128) if p.ctx else (L // 128)) for s_ in range(p.nseq)) + 8 * (ntile + 1))
                import os as _os
                if _os.environ.get("MK_DRIVE", "il") == "seq":
                    for _ in ssm_phase():
                        pass
                    for _ in attn_phase():
                        pass
                else:
                    drive(ssm_phase(), attn_phase(), max(1, int(round(n_att / n_ssm))))

                if l == 0:
                    for s_ in range(p.nseq):
                        for col in (0, p.L + 1):
                            DMA(S_["h2"].rearrange("(c p) s n -> p c s n", p=128)[:, :, s_, col:col + 1], zeros_b[:, 0:8].rearrange("p (c o) -> p c o", o=1),
                                [rs("zeros_b")], [rs("h2_%s_%d" % (pn, s_))])
                DMA(WB[:, 0:8192], w_out_b[l], [rs("w_out_b%d" % l)], [RW])
                woT = WB[:, 0:8192].rearrange("p (k m) -> p k m", k=8)

                def p3_bufs(t_):
                    if t_ % 2 == 0:
                        return hb[1], rs("hb1")
                    return U1[:, 0:4096].rearrange("p (c n) -> p c n", c=8), rs("U1")

                def p3_load(t_):
                    c0_ = t_ * 512
                    n_ = min(512, N - c0_)
                    Y_, RY_ = p3_bufs(t_)
                    DMA(Y_[:, 0:2, 0:n_], S_["ya"].rearrange("(c p) n -> p c n", p=128)[:, :, c0_:c0_ + n_], [tR("ya", t_)], [RY_])
                    DMA(Y_[:, 2:6, 0:n_], S_["yb"].rearrange("(c p) n -> p c n", p=128)[:, :, c0_:c0_ + n_],
                        [tR("yb%d" % hh, t_) for hh in range(8)], [RY_])
                    DMA(Y_[:, 6:8, 0:n_], S_["yc"].rearrange("(c p) n -> p c n", p=128)[:, :, c0_:c0_ + n_], [tR("yc", t_)], [RY_])
                    DMA(xt[t_ % 2][:, :, 0:n_], x_src.rearrange("(c p) n -> p c n", p=128)[:, :, c0_:c0_ + n_],
                        [tR("x", t_)] if l > 0 else [], [rs("xt%d" % (t_ % 2))])
                for t in range(ntile):
                    c0 = t * 512
                    n = min(512, N - c0)
                    YC8, RY8 = p3_bufs(t)
                    X = xt[t % 2]
                    RX = rs("xt%d" % (t % 2))
                    if t == 0:
                        p3_load(0)
                    if t + 1 < ntile:
                        p3_load(t + 1)
                    SQ = hb[0]
                    RSQ = rs("hb0")
                    ACT(SQ[:, 0:4, 0:n], YC8[:, 2:6, 0:n], AF.Square, [RY8], [RSQ])
                    for j in range(4):
                        MM(pbs[0][:, 0:n], ones_b[:], SQ[:, j, 0:n], j == 0, j == 3, [RSQ, rs("ones_b")], [pbR[0]])
                    rstd_from_ss(pbs[0][:, 0:n], rstd[:, 0:n], 1.0 / 512, [pbR[0]], [rs("rstd")])
                    for j in range(4):
                        STT(YC8[:, 2 + j, 0:n], YC8[:, 2 + j, 0:n], V("won", 2 + j), rstd[:, 0:n], ALU.mult, ALU.mult,
                            [RY8, Rv, rs("rstd")], [RY8])
                    X1 = X
                    RX1 = RX
                    for o in range(8):
                        b = gbank()
                        for k in range(8):
                            MM(pbs[b][:, 0:n], woT[:, k, o * 128:(o + 1) * 128], YC8[:, k, 0:n], k == 0, k == 7, [RW, RY8], [pbR[b]])
                        STT(X1[:, o, 0:n], pbs[b][:, 0:n], MOD(2, o), X[:, o, 0:n], ALU.mult, ALU.add, [pbR[b], Rm, RX], [RX1])
                    DMA(S_["xa"].rearrange("(c p) n -> p c n", p=128)[:, :, c0:c0 + n], X1[:, :, 0:n], [RX1], [tR("xa", t)])
                    ACT(SQ[:, :, 0:n], X1[:, :, 0:n], AF.Square, [RX1], [RSQ])
                    for c in range(8):
                        MM(pbs[0][:, 0:n], ones_b[:], SQ[:, c, 0:n], c == 0, c == 7, [RSQ, rs("ones_b")], [pbR[0]])
                    rstd_from_ss(pbs[0][:, 0:n], rstd[:, 0:n], 1.0 / D, [pbR[0]], [rs("rstd")])
                    for c in range(8):
                        tf = tmpf[c % 2]
                        Rtf = rs("tmpf%d" % (c % 2))
                        STT(tf[:, 0:n], X1[:, c, 0:n], MOD(4, c), rstd[:, 0:n], ALU.mult, ALU.mult, [RX1, Rm, rs("rstd")], [Rtf])
                        ACT(SQ[:, c, 0:n], tf[:, 0:n], AF.Identity, [Rtf, Rm], [RSQ], bias=MOD(3, c))
                    for s in range(p.nseq):
                        lo, hi = max(c0, s * L), min(c0 + n, (s + 1) * L)
                        if lo < hi:
                            DMA(S_["h2"].rearrange("(c p) s n -> p c s n", p=128)[:, :, s, 1 + lo - s * L:1 + hi - s * L],
                                SQ[:, :, lo - c0:hi - c0], [RSQ], [rs("h2_%s_%d" % (pn, s))])

                ACTB = U1[:, 0:NF * 512].rearrange("p (f n) -> p f n", f=NF)
                RU = rs("U1")
                ftile = 0
                ftiles = [(s, j0) for s in range(p.nseq) for j0 in range(0, L, 510)]

                def ffn_load(fi):
                    s, j0 = ftiles[fi]
                    no = min(510, L - j0)
                    ncol = no + 2
                    g0 = s * L + j0
                    tl = sorted(set([g0 // 512, (g0 + no - 1) // 512]))
                    DMA(hb[fi % 2][:, :, 0:ncol], S_["h2"].rearrange("(c p) s n -> p c s n", p=128)[:, :, s, j0:j0 + ncol],
                        [rs("h2_%s_%d" % (pn, s))], [rs("hb%d" % (fi % 2))])
                    DMA(xt[fi % 2][:, :, 0:no], S_["xa"].rearrange("(c p) n -> p c n", p=128)[:, :, g0:g0 + no],
                        [tR("xa", t) for t in tl], [rs("xt%d" % (fi % 2))])

                ffn_load(0)
                for (s, j0) in ftiles:
                    if True:
                        no = min(510, L - j0)
                        ncol = no + 2
                        H2 = hb[ftile % 2]
                        RH2 = rs("hb%d" % (ftile % 2))
                        g0 = s * L + j0
                        X1 = xt[ftile % 2]
                        RX1 = rs("xt%d" % (ftile % 2))
                        tl = sorted(set([g0 // 512, (g0 + no - 1) // 512]))
                        for f in range(NF):
                            ri = f % 3
                            rb = ring_up[ri]
                            rR = rs("rup%d" % ri)
                            DMA(rb[:], w_up_b[l, f], [rs("w_up_b%d" % l)], [rR])
                            rbv = rb[:].rearrange("p (k m) -> p k m", k=8)
                            bg = 1 + (f % 2)
                            bv = 3 + (f % 2)
                            for k in range(8):
                                MM(pbs[bg][:, 0:ncol], rbv[:, k, 0:128], H2[:, k, 0:ncol], k == 0, k == 7, [rR, RH2], [pbR[bg]])
                            for k in range(8):
                                MM(pbs[bv][:, 0:ncol], rbv[:, k, 128:256], H2[:, k, 0:ncol], k == 0, k == 7, [rR, RH2], [pbR[bv]])
                            outs = []
                            for (bb, chn, tfi) in ((bg, f, 0), (bv, NF + f, 1)):
                                tf = tmpf[tfi]
                                Rtf = rs("tmpf%d" % tfi)
                                ps_ = pbs[bb]
                                ACT(tf[:, 0:no], ps_[:, 1:1 + no], AF.Identity, [pbR[bb], Rv], [Rtf],
                                    scale=V("convw", 44 + chn), bias=V("convb", chn))
                                STT(tf[:, 0:no], ps_[:, 0:no], V("convw", chn), tf[:, 0:no], ALU.mult, ALU.add, [pbR[bb], Rv, Rtf], [Rtf])
                                STT(tf[:, 0:no], ps_[:, 2:2 + no], V("convw", 88 + chn), tf[:, 0:no], ALU.mult, ALU.add,
                                    [pbR[bb], Rv, Rtf], [Rtf])
                            ACT(tmpf[2][:, 0:no], tmpf[0][:, 0:no], AF.Silu, [rs("tmpf0")], [rs("tmpf2")])
                            TT("pool", ACTB[:, f, 0:no], tmpf[2][:, 0:no], tmpf[1][:, 0:no], ALU.mult, [rs("tmpf2"), rs("tmpf1")], [RU])
                        if ftile + 1 < len(ftiles):
                            ffn_load(ftile + 1)
                        for o in range(8):
                            b = 5 + (o % 2)
                            for hf in range(2):
                                ri = (2 * o + hf) % 3
                                rb = ring_up[ri]
                                rR = rs("rup%d" % ri)
                                DMA(rb[:, 0:11 * 128], w_down_b[l, o][:, hf * 1408:(hf + 1) * 1408], [rs("w_down_b%d" % l)], [rR])
                                rbv = rb[:, 0:11 * 128].rearrange("p (f m) -> p f m", f=11)
                                for fi in range(11):
                                    f = hf * 11 + fi
                                    MM(pbs[b][:, 0:no], rbv[:, fi, :], ACTB[:, f, 0:no], f == 0, f == NF - 1, [rR, RU], [pbR[b]])
                            STT(X1[:, o, 0:no], pbs[b][:, 0:no], MOD(5, o), X1[:, o, 0:no], ALU.mult, ALU.add, [pbR[b], Rm, RX1], [RX1])
                        DMA(x_dst.rearrange("(c p) n -> p c n", p=128)[:, :, g0:g0 + no], X1[:, :, 0:no], [RX1],
                            [tR("x", t) for t in tl] + [rs("xout_" + pn)])
                        ftile += 1

        P.op("sp", lambda e: e.nop(), reads=[rs("xout_p"), rs("xout_s"), rs("out_ckv"), rs("out_kr"), rs("out_h")])
        P.emit(st)
        build_program.stats = P.stats
    return nc


def _f(a):
    return np.ascontiguousarray(np.asarray(a, dtype=np.float32))


def _chunkcols(v):
    v = np.asarray(v, np.float32)
    return np.ascontiguousarray(v.reshape(-1, 128).T)


def _ktile(w):
    K, M = w.shape
    return np.ascontiguousarray(w.reshape(K // 128, 128, M).transpose(1, 0, 2).reshape(128, -1))


def _rope_tables(L):
    rows = L // GRID_W
    row = np.repeat(np.arange(rows, dtype=np.float32), GRID_W)
    col = np.tile(np.arange(GRID_W, dtype=np.float32), rows)
    n_freq = 8
    inv = (np.float32(10000.0) ** (-np.arange(n_freq, dtype=np.float32) / np.float32(n_freq))).astype(np.float32)
    ang = np.concatenate([row[:, None] * inv, col[:, None] * inv], axis=-1).astype(np.float32)
    c, s = np.cos(ang).astype(np.float32), np.sin(ang).astype(np.float32)
    C = np.concatenate([c.T, c.T], axis=0)
    S = np.concatenate([s.T, s.T], axis=0)
    return _f(C), _f(S)


def _shared_inputs(I, dec_seq):
    out = {}
    ropeC, ropeS = _rope_tables(dec_seq)
    out["ropeC"], out["ropeS"] = ropeC, ropeS
    consts = np.zeros((128, 256), np.float32)
    for m in range(16):
        consts[m + 16, m] = -1.0
    for m in range(16, 32):
        consts[m - 16, m] = 1.0
    for k in range(32):
        consts[k, 32 + k] = 1.0
    out["consts"] = consts
    w_in = _f(I["w_in"])
    perm = np.concatenate([np.arange(0, 640), np.arange(672, 928), np.arange(928, 1184), np.arange(640, 672)])
    out["w_in_t"] = np.stack([_ktile(w_in[l][:, perm]) for l in range(DEPTH)])
    out["w_out_t"] = np.stack([_ktile(_f(I["w_out"])[l]) for l in range(DEPTH)])
    hperm = np.concatenate([np.arange(64, 96), np.arange(0, 64)])
    qperm = np.concatenate([h * 96 + hperm for h in range(8)])
    out["w_uq_t"] = np.stack([_ktile(_f(I["w_uq"])[l][:, qperm]) for l in range(DEPTH)])
    w_ukv = _f(I["w_ukv"])
    wk = np.zeros((DEPTH, 128, 8, 96), np.float32)
    wv = np.zeros((DEPTH, 128, 512), np.float32)
    for h in range(8):
        wk[:, :, h, 32:96] = w_ukv[:, :, h * 128:h * 128 + 64]
        wv[:, :, h * 64:(h + 1) * 64] = w_ukv[:, :, h * 128 + 64:h * 128 + 128]
    out["w_ukvk_t"] = wk.reshape(DEPTH, 128, 768)
    out["w_ukvv_t"] = wv
    out["w_glu_t"] = np.stack([_ktile(_f(I["ssm_w_glu"])[l]) for l in range(DEPTH)])
    w_mod = _f(I["w_mod"])
    out["w_mod_t"] = np.ascontiguousarray(
        w_mod.reshape(DEPTH, 8, 128, 24, 256).transpose(0, 3, 2, 1, 4).reshape(DEPTH, 24, 128, 2048))
    w_up = _f(I["ffn_w_up"])
    g = w_up[:, :, :DFF].reshape(DEPTH, 8, 128, NF, 128)
    v = w_up[:, :, DFF:].reshape(DEPTH, 8, 128, NF, 128)
    gv = np.stack([g, v], axis=4)
    out["w_up_t"] = np.ascontiguousarray(gv.transpose(0, 3, 2, 1, 4, 5).reshape(DEPTH, NF, 128, 2048))
    w_dn = _f(I["ffn_w_down"])
    out["w_down_t"] = np.ascontiguousarray(
        w_dn.reshape(DEPTH, NF, 128, 8, 128).transpose(0, 3, 2, 1, 4).reshape(DEPTH, 8, 128, NF * 128))
    Bre, Bim = _f(I["ssm_b_re"]), _f(I["ssm_b_im"])
    Cre, Cim = _f(I["ssm_c_re"]), _f(I["ssm_c_im"])
    Bp = np.zeros((DEPTH, 128, 16, 2, 128), np.float32)
    Cp = np.zeros((DEPTH, 128, 16, 2, 128), np.float32)
    for gp in range(8):
        for d in range(2):
            gd = gp * 2 + d
            for glo in range(2):
                gg = 2 * gp + glo
                k0 = 16 * (gg % 8)
                for r, (B_, C_) in enumerate(((Bre, Cre), (Bim, Cim))):
                    Bp[:, k0:k0 + 16, gd, r, glo * 64:(glo + 1) * 64] = B_[:, d, gg].transpose(0, 2, 1)
                    Cp[:, glo * 64:(glo + 1) * 64, gd, r, k0:k0 + 16] = C_[:, d, gg].transpose(0, 2, 1)
    out["ssmB_t"] = Bp.reshape(DEPTH, 128, 4096)
    out["ssmC_t"] = Cp.reshape(DEPTH, 128, 4096)
    gw = _f(I["gmlp_w_s"])
    out["gmlp_wsT"] = np.ascontiguousarray(gw.transpose(0, 3, 1, 2).reshape(DEPTH, 128, 512))
    vecs = np.zeros((DEPTH, 128, NV), np.float32)

    def put(name, l, arr):
        arr = np.asarray(arr, np.float32)
        vecs[l, :arr.shape[0], VOFF[name]:VOFF[name] + arr.shape[1]] = arr

    for l in range(DEPTH):
        put("bmod", l, _chunkcols(I["b_mod"][l]))
        put("ssmd", l, _chunkcols(I["ssm_d"][l]))
        put("qan", l, _chunkcols(I["q_a_norm"][l]))
        put("kvan", l, _chunkcols(I["kv_a_norm"][l]))
        put("qn", l, _f(I["q_norm"])[l][hperm][:, None])
        put("kn", l, _f(I["k_norm"])[l][hperm][:, None])
        put("won", l, _chunkcols(I["w_out_norm"][l]))
        cw = _f(I["ffn_conv_w"])[l]
        put("convw", l, np.concatenate([_chunkcols(cw[t]) for t in range(3)], axis=1))
        put("convb", l, _chunkcols(I["ffn_conv_b"][l]))
        bs = _f(I["gmlp_b_s"])[l]
        gb = np.zeros((128, 2, 128), np.float32)
        for j in range(2):
            gb[0:64, j, :] = bs[2 * j][None, :]
            gb[64:128, j, :] = bs[2 * j + 1][None, :]
        put("gbias", l, gb.reshape(128, 256))
        put("gvn", l, np.broadcast_to(_f(I["gmlp_v_norm"])[l][None, :], (128, 256)))
        for nm, key in (("are", "ssm_a_re"), ("aim", "ssm_a_im")):
            a = _f(I[key])[l]
            t = np.zeros((128, 16), np.float32)
            for gp in range(8):
                for d in range(2):
                    for glo in range(2):
                        t[glo * 64:(glo + 1) * 64, gp * 2 + d] = a[d, 2 * gp + glo]
            put(nm, l, t)
        ld = _f(I["ssm_log_dt"])[l]
        t = np.zeros((128, 16), np.float32)
        for gp in range(8):
            for d in range(2):
                for glo in range(2):
                    t[glo * 64:(glo + 1) * 64, gp * 2 + d] = ld[d, 2 * gp + glo]
        put("ldt", l, t)
    out["vecs"] = vecs
    return out


_PROG_CACHE = {}


def kernel(x_prompt, x_sample, cache_ckv, cache_krope, state_ssm_re, state_ssm_im, c, c_ctx, **W):
    I = dict(W)
    dec_seq = x_sample.shape[1]
    if dec_seq not in _PROG_CACHE:
        _PROG_CACHE[dec_seq] = build_program(dec_seq)
    nc = _PROG_CACHE[dec_seq]
    shared = _shared_inputs(I, dec_seq)
    xp = _f(x_prompt)
    xs = _f(x_sample)
    B = xp.shape[0]
    n_s = xs.shape[0]
    in_maps = []
    for core in range(N_CORES):
        m = dict(shared)
        seqs = xp[core * NSEQ_P:(core + 1) * NSEQ_P]
        m["xT_p"] = np.ascontiguousarray(seqs.reshape(NSEQ_P * SEQ, D).T)
        cond = np.zeros((D, 2), np.float32)
        cond[:, 0] = _f(c_ctx)
        if core < n_s:
            m["xT_s"] = np.ascontiguousarray(xs[core].T)
            cond[:, 1] = _f(c)[core]
            m["cache_ckvT"] = np.ascontiguousarray(_f(cache_ckv)[core].transpose(0, 2, 1))
            m["cache_krT"] = np.ascontiguousarray(_f(cache_krope)[core].transpose(0, 2, 1))
            h0 = np.zeros((DEPTH, 128, 16, 2), np.float32)
            for r, st_ in enumerate((_f(state_ssm_re)[core], _f(state_ssm_im)[core])):
                for gp in range(8):
                    for d in range(2):
                        for glo in range(2):
                            h0[:, glo * 64:(glo + 1) * 64, gp * 2 + d, r] = st_[:, d, 2 * gp + glo]
            m["h0"] = h0
        else:
            m["xT_s"] = np.zeros((D, dec_seq), np.float32)
            m["cache_ckvT"] = np.zeros((DEPTH, 128, PAST), np.float32)
            m["cache_krT"] = np.zeros((DEPTH, 32, PAST), np.float32)
            m["h0"] = np.zeros((DEPTH, 128, 16, 2), np.float32)
        m["condT"] = np.ascontiguousarray(cond.reshape(8, 128, 2).transpose(1, 0, 2))
        in_maps.append(m)
    res = run_bass_kernel_spmd(nc, in_maps, core_ids=list(range(N_CORES)))
    R_ = res.results
    y_prompt = np.zeros((B, SEQ, D), np.float32)
    y_sample = np.zeros((n_s, dec_seq, D), np.float32)
    new_ckv = np.zeros((B, DEPTH, SEQ, 128), np.float32)
    new_kr = np.zeros((B, DEPTH, SEQ, 32), np.float32)
    new_re = np.zeros((B, DEPTH, 2, 16, 64), np.float32)
    new_im = np.zeros((B, DEPTH, 2, 16, 64), np.float32)
    for core in range(N_CORES):
        r = R_[core]
        y_prompt[core * NSEQ_P:(core + 1) * NSEQ_P] = r["yT_p"].T.reshape(NSEQ_P, SEQ, D)
        if core < n_s:
            y_sample[core] = r["yT_s"].T
        ck = r["new_ckvT"]
        kr = r["new_krT"]
        nh = r["new_h"]
        for s in range(NSEQ_P):
            b = core * NSEQ_P + s
            new_ckv[b] = ck[:, :, s * SEQ:(s + 1) * SEQ].transpose(0, 2, 1)
            new_kr[b] = kr[:, :, s * SEQ:(s + 1) * SEQ].transpose(0, 2, 1)
            for gp in range(8):
                for d in range(2):
                    for glo in range(2):
                        new_re[b, :, d, 2 * gp + glo] = nh[:, s, glo * 64:(glo + 1) * 64, gp * 2 + d, 0]
                        new_im[b, :, d, 2 * gp + glo] = nh[:, s, glo * 64:(glo + 1) * 64, gp * 2 + d, 1]
    return (y_prompt, y_sample, new_ckv, new_kr, new_re, new_im)
```
